# Optimizing a Trainium2 kernel written in Bass

```python
import jax, jax.numpy as jnp
from jax import lax
import numpy as np

D_MODEL = 2048
BATCH = 4
SEQ = 4096
DEPTH = 2

N_MIXERS = 2
N_FOURIER_LAYERS = (DEPTH + 1) // 2
N_ATTN_LAYERS = DEPTH // 2
FOURIER_GROUPS = 8
FOURIER_GROUP_WIDTH = D_MODEL // FOURIER_GROUPS
HEAD_DIM = 64
N_Q_HEADS = D_MODEL // HEAD_DIM
N_KV_HEADS = N_Q_HEADS // 4
Q_PER_KV = N_Q_HEADS // N_KV_HEADS
QKV_WIDTH = (N_Q_HEADS + 2 * N_KV_HEADS) * HEAD_DIM
WINDOW = 128
BLOCK = 128
ROPE_THETA = 10000.0
PEER_HEADS = 8
N_KEYS = 128
N_EXPERTS = N_KEYS * N_KEYS
PEER_QUERY_DIM = 256
PEER_HALF = PEER_QUERY_DIM // 2
PEER_TOPK = 16
PEER_CHUNK = 128
EPS = 1e-6
NEG_INF = -1e30

kernel_name = "hybrid_fnet_swa_peer_encoder"


def rmsnorm(x, g):
    xf = x.astype(jnp.float32)
    y = xf * lax.rsqrt(jnp.mean(xf * xf, axis=-1, keepdims=True) + EPS)
    return (y * g.astype(jnp.float32)).astype(x.dtype)


def fourier_mixer(h, w_o):
    B, S, D = h.shape
    hg = h.astype(jnp.float32).reshape(B, S, FOURIER_GROUPS, FOURIER_GROUP_WIDTH)
    hg = jnp.transpose(hg, (0, 2, 1, 3))
    y = jnp.fft.fft2(hg, axes=(-2, -1), norm='ortho').real
    y = jnp.transpose(y, (0, 2, 1, 3)).reshape(B, S, D).astype(h.dtype)
    return y @ w_o


def rope(x, pos):
    hd = x.shape[-1]
    inv_freq = ROPE_THETA ** (-jnp.arange(0, hd, 2, dtype=jnp.float32) / hd)
    ang = pos[:, None] * inv_freq[None, :]
    cos = jnp.cos(ang)[None, :, None, :]
    sin = jnp.sin(ang)[None, :, None, :]
    xf = x.astype(jnp.float32)
    x1, x2 = xf[..., : hd // 2], xf[..., hd // 2:]
    out = jnp.concatenate([x1 * cos - x2 * sin, x2 * cos + x1 * sin], axis=-1)
    return out.astype(x.dtype)


def _band_windows(t, nb):
    B, S, KV, hd = t.shape
    tp = jnp.pad(t, ((0, 0), (BLOCK, BLOCK), (0, 0), (0, 0))).reshape(B, nb + 2, BLOCK, KV, hd)
    return jnp.concatenate([tp[:, :-2], tp[:, 1:-1], tp[:, 2:]], axis=2)


def windowed_gqa_mixer(h, w_qkv, w_o, sinks):
    B, S, D = h.shape
    nb = S // BLOCK
    qkv = h @ w_qkv
    q = qkv[..., : N_Q_HEADS * HEAD_DIM].reshape(B, S, N_Q_HEADS, HEAD_DIM)
    k = qkv[..., N_Q_HEADS * HEAD_DIM:(N_Q_HEADS + N_KV_HEADS) * HEAD_DIM].reshape(B, S, N_KV_HEADS, HEAD_DIM)
    v = qkv[..., (N_Q_HEADS + N_KV_HEADS) * HEAD_DIM:].reshape(B, S, N_KV_HEADS, HEAD_DIM)
    pos = jnp.arange(S, dtype=jnp.float32)
    q = rope(q, pos)
    k = rope(k, pos)
    qb = q.reshape(B, nb, BLOCK, N_KV_HEADS, Q_PER_KV, HEAD_DIM)
    kw = _band_windows(k, nb)
    vw = _band_windows(v, nb)
    scale = HEAD_DIM ** -0.5
    scores = jnp.einsum('bnqkgd,bnskd->bnkgqs', qb, kw).astype(jnp.float32) * scale
    qi = jnp.arange(BLOCK)[:, None]
    si = jnp.arange(3 * BLOCK)[None, :]
    band = jnp.abs(si - BLOCK - qi) <= WINDOW
    key_pos = jnp.arange(nb)[:, None] * BLOCK - BLOCK + jnp.arange(3 * BLOCK)[None, :]
    inside = (key_pos >= 0) & (key_pos < S)
    mask = band[None, :, :] & inside[:, None, :]
    scores = jnp.where(mask[None, :, None, None, :, :], scores, NEG_INF)
    sink = sinks.astype(jnp.float32).reshape(N_KV_HEADS, Q_PER_KV)[None, None, :, :, None, None]
    m = jnp.maximum(jnp.max(scores, axis=-1, keepdims=True), sink)
    p = jnp.exp(scores - m)
    p = p / (jnp.sum(p, axis=-1, keepdims=True) + jnp.exp(sink - m))
    out = jnp.einsum('bnkgqs,bnskd->bnqkgd', p.astype(vw.dtype), vw).reshape(B, S, D)
    return out @ w_o


def peer_mixer(h, w_q, sub_keys, u, v):
    B, S, D = h.shape
    T = B * S

    def chunk_fn(xc):
        C = xc.shape[0]
        q = (xc @ w_q).reshape(C, PEER_HEADS, 2, PEER_HALF)
        s = jnp.einsum('chpd,hpkd->chpk', q, sub_keys).astype(jnp.float32)
        s1, i1 = lax.top_k(s[:, :, 0], PEER_TOPK)
        s2, i2 = lax.top_k(s[:, :, 1], PEER_TOPK)
        cand = (s1[..., :, None] + s2[..., None, :]).reshape(C, PEER_HEADS, PEER_TOPK * PEER_TOPK)
        cidx = (i1[..., :, None] * N_KEYS + i2[..., None, :]).reshape(C, PEER_HEADS, PEER_TOPK * PEER_TOPK)
        top, sel = lax.top_k(cand, PEER_TOPK)
        eidx = jnp.take_along_axis(cidx, sel, axis=-1)
        g = jax.nn.softmax(top, axis=-1)
        u_sel = jnp.take(u, eidx, axis=0)
        a = jax.nn.gelu(jnp.einsum('chkd,cd->chk', u_sel, xc).astype(jnp.float32))
        v_sel = jnp.take(v, eidx, axis=0)
        return jnp.einsum('chk,chkd->cd', (g * a).astype(xc.dtype), v_sel)

    out = lax.map(chunk_fn, h.reshape(T // PEER_CHUNK, PEER_CHUNK, D))
    return out.reshape(B, S, D)


def setup_inputs(seed: int = 0) -> dict:
    key = jax.random.key(seed)
    ks = jax.random.split(key, 13)
    f32 = jnp.float32
    sd = D_MODEL ** -0.5
    x = jax.random.normal(ks[0], (BATCH, SEQ, D_MODEL), f32)
    mix_norm = 1.0 + 0.02 * jax.random.normal(ks[1], (DEPTH, D_MODEL), f32)
    ffn_norm = 1.0 + 0.02 * jax.random.normal(ks[2], (DEPTH, D_MODEL), f32)
    fourier_w_o = jax.random.normal(ks[3], (N_FOURIER_LAYERS, D_MODEL, D_MODEL), f32) * sd
    attn_w_qkv = jax.random.normal(ks[4], (N_ATTN_LAYERS, D_MODEL, QKV_WIDTH), f32) * sd
    attn_w_o = jax.random.normal(ks[5], (N_ATTN_LAYERS, D_MODEL, D_MODEL), f32) * sd
    attn_sinks = 0.5 * jax.random.normal(ks[6], (N_ATTN_LAYERS, N_Q_HEADS), f32)
    peer_w_q = jax.random.normal(ks[7], (DEPTH, D_MODEL, PEER_HEADS * PEER_QUERY_DIM), f32) * sd
    peer_sub_keys = jax.random.normal(ks[8], (DEPTH, PEER_HEADS, 2, N_KEYS, PEER_HALF), f32) * PEER_HALF ** -0.5
    peer_u = jax.random.normal(ks[9], (DEPTH, N_EXPERTS, D_MODEL), f32) * sd
    peer_v = jax.random.normal(ks[10], (DEPTH, N_EXPERTS, D_MODEL), f32) * (PEER_HEADS * PEER_TOPK) ** -0.5
    final_norm = 1.0 + 0.02 * jax.random.normal(ks[11], (D_MODEL,), f32)
    return {"x": x, "mix_norm": mix_norm, "ffn_norm": ffn_norm, "fourier_w_o": fourier_w_o,
            "attn_w_qkv": attn_w_qkv, "attn_w_o": attn_w_o, "attn_sinks": attn_sinks,
            "peer_w_q": peer_w_q, "peer_sub_keys": peer_sub_keys, "peer_u": peer_u,
            "peer_v": peer_v, "final_norm": final_norm}


def reference(x, mix_norm, ffn_norm, fourier_w_o, attn_w_qkv, attn_w_o, attn_sinks,
              peer_w_q, peer_sub_keys, peer_u, peer_v, final_norm):
    for i in range(DEPTH):
        h = rmsnorm(x, mix_norm[i])
        j = i // N_MIXERS
        if i % N_MIXERS == 0:
            x = x + fourier_mixer(h, fourier_w_o[j])
        else:
            x = x + windowed_gqa_mixer(h, attn_w_qkv[j], attn_w_o[j], attn_sinks[j])
        h = rmsnorm(x, ffn_norm[i])
        x = x + peer_mixer(h, peer_w_q[i], peer_sub_keys[i], peer_u[i], peer_v[i])
    return rmsnorm(x, final_norm)
```

```python
import numpy as np
import ml_dtypes
import concourse.bass as bass
import concourse.mybir as mybir
from concourse.bass_utils import run_bass_kernel_spmd

F32 = mybir.dt.float32
BF16 = mybir.dt.bfloat16
U32 = mybir.dt.uint32
ALU = mybir.AluOpType
AF = mybir.ActivationFunctionType
AX = mybir.AxisListType

D = 2048
NE = 16384
EPS = 1e-6
ENGS = ('pe', 'act', 'dve', 'pool', 'sp')


class Buf:
    __slots__ = ('name', 'w', 'r')

    def __init__(self, name):
        self.name = name
        self.w = None
        self.r = {}


class Prog:
    def __init__(self, nc):
        self.nc = nc
        self.q = {e: [] for e in ENGS}
        self.cnt = {e: 0 for e in ENGS}
        self.dcnt = {}
        self.sem = {}
        self.waited = {e: {} for e in ENGS}

    def _deps(self, r, w):
        deps = {}

        def add(tok):
            if tok is None:
                return
            k = tok[:2]
            if deps.get(k, 0) < tok[2]:
                deps[k] = tok[2]
        for b in r:
            add(b.w)
        for b in w:
            add(b.w)
            for k, v in b.r.items():
                add((k[0], k[1], v))
        return deps

    def _commit(self, r, w, tok):
        k = tok[:2]
        for b in r:
            if b.r.get(k, 0) < tok[2]:
                b.r[k] = tok[2]
        for b in w:
            b.w = tok
            b.r = {}

    def op(self, eng, fn, r=(), w=()):
        deps = self._deps(r, w)
        self.cnt[eng] += 1
        tok = ('e', eng, self.cnt[eng])
        self.q[eng].append((fn, deps, tok[:2], 1))
        self._commit(r, w, tok)
        return tok

    def dma(self, parts, key, r=(), w=()):
        deps = self._deps(r, w)
        c = self.dcnt.get(key, 0)
        for eng, fn in parts:
            c += 16
            self.q[eng].append((fn, deps, ('d', key), 16))
        self.dcnt[key] = c
        tok = ('d', key, c)
        self._commit(r, w, tok)
        return tok

    def get_sem(self, k, stack):
        if k not in self.sem:
            self.sem[k] = stack.enter_context(self.nc.semaphore("s_%s_%s" % (k[0], str(k[1]))))
        return self.sem[k]

    def flush(self, stack, barrier=True):
        nc = self.nc
        for e in ENGS:
            self.get_sem(('e', e), stack)
        for e in ENGS:
            for fn, deps, tk, inc in self.q[e]:
                self.get_sem(tk, stack)
        final = {('e', e): self.cnt[e] for e in ENGS}
        for k, v in self.dcnt.items():
            final[('d', k)] = v
        queues = self.q
        self.q = {e: [] for e in ENGS}

        def run(eng_name, eh):
            waited = self.waited[eng_name]
            for fn, deps, tk, inc in queues[eng_name]:
                for k, v in deps.items():
                    if k == ('e', 'pe') and eng_name == 'pe':
                        continue
                    if waited.get(k, 0) >= v:
                        continue
                    eh.wait_ge(self.sem[k], v)
                    waited[k] = v
                ins = fn(eh)
                ins.then_inc(self.sem[tk], inc)
            if barrier:
                for k, v in final.items():
                    if v > 0 and waited.get(k, 0) < v and k != ('e', eng_name):
                        eh.wait_ge(self.sem[k], v)
                        waited[k] = v

        with nc.Block() as block:
            @block.tensor
            def _(eh):
                run('pe', eh)

            @block.scalar
            def _(eh):
                run('act', eh)

            @block.vector
            def _(eh):
                run('dve', eh)

            @block.gpsimd
            def _(eh):
                run('pool', eh)

            @block.sync
            def _(eh):
                run('sp', eh)


def mkap(t, offset, dims):
    base = t[:] if not isinstance(t, bass.AP) else t
    return bass.AP(base.tensor, base.offset + offset, [list(base.ap[0])] + [list(d) for d in dims])


class Converter:
    def __init__(self, nc, P, stack, uT, v, u_bf, v_bf, wq, wq_bf, ncalls, R=4):
        self.P = P
        self.jobs = []
        for dc in range(16):
            self.jobs.append((wq[dc * 128:(dc + 1) * 128, :].rearrange("p (h c) -> p h c", c=128),
                              wq_bf[:, :, dc, :].rearrange("h p c -> p h c")))
        for i in range(128):
            self.jobs.append((uT[i], u_bf[i]))
            self.jobs.append((v[i * 128:(i + 1) * 128, :], v_bf[i * 128:(i + 1) * 128, :]))
        self.i = 0
        self.rate = len(self.jobs) / float(ncalls)
        self.accum = 0.0

    def step(self, n):
        P = self.P
        for _ in range(n):
            if self.i >= len(self.jobs):
                return
            src, dst = self.jobs[self.i]
            k = self.i % 4
            self.i += 1
            P.dma([('pool', lambda e, src=src, dst=dst: e.dma_start(out=dst, in_=src))], 'cv%d' % k)

    def tick(self):
        self.accum += self.rate
        n = int(self.accum)
        self.accum -= n
        self.step(n)

    def finish(self):
        self.step(len(self.jobs))


def peer_phase(nc, P, stack, NT, x_in, x_out, gnorm, wq, skT, uT, v, wg, ident_d, iota_d,
               final_g=None, conv_args=None):
    pfx = "pr%d_" % P.cnt['pe']
    sb = lambda name, shape, dt: stack.enter_context(nc.sbuf_tensor(pfx + name, shape, dt))
    ps = lambda name, shape, dt: stack.enter_context(nc.psum_tensor(pfx + name, shape, dt))
    TB = 4
    SC = 8
    NUB, NVB, NGB = 3, 12, 3
    acc = sb("acc", [128, TB, D], F32)
    g_bc = sb("g_bc", [128, D], F32)
    fg_bc = sb("fg_bc", [128, D], F32) if final_g is not None else None
    ident = sb("ident", [128, 128], F32)
    iota = sb("iota", [128, 128], F32)
    xt = sb("xt", [128, D], F32)
    i16 = sb("i16", [128, 16], F32)
    cif = sb("cif", [128, 8, 16], F32)
    h_f = sb("h_f", [128, D], F32)
    hT = sb("hT", [128, 16, TB * 128], BF16)
    wq_bf = [sb("wq_bf%d" % i, [128, 16, 128], BF16) for i in range(2)]
    sk_bf = sb("sk_bf", [128, 16, 128], BF16)
    s_sb = sb("s_sb", [128, 16, 128], F32)
    s2_sb = sb("s2_sb", [128, 16, 128], F32)
    cand = s_sb[:].rearrange("p (h a) b -> p h (a b)", h=8)
    cand2 = s2_sb[:].rearrange("p (h a) b -> p h (a b)", h=8)
    tv = sb("tv", [128, 16, 16], F32)
    ti = sb("ti", [128, 16, 16], U32)
    tif = sb("tif", [128, 16, 16], F32)
    cv = sb("cv", [128, 8, 16], F32)
    ci = sb("ci", [128, 8, 16], U32)
    jf = sb("jf", [128, 8, 16], F32)
    kf = sb("kf", [128, 8, 16], F32)
    oh = xt[:].rearrange("p (a b) -> p a b", b=16)
    sel = sb("sel", [128, 3, 128], F32)
    selT = sb("selT", [128, 3, 128], F32)
    ssq = sb("ssq", [128, 8], F32)
    zz = sb("zz", [128, 16], F32)
    uT_bf = [sb("uT_bf%d" % i, [128, 16, 128], BF16) for i in range(NUB)]
    v_all = sb("v_all", [128, NVB, D], BF16)
    v_bf = [v_all[:, i, :] for i in range(NVB)]
    A_q = [v_all[:, 2 * i:2 * i + 2, :].rearrange("p a (b c) -> p (a b) c", c=128) for i in range(2)]
    B_q = [v_all[:, 4 + 2 * i:6 + 2 * i, :].rearrange("p a (b c) -> p (a b) c", c=128) for i in range(2)]
    Wst = acc[:].rearrange("p a b -> p (a b)").bitcast(BF16).rearrange("p (i t) -> p i t", i=128)
    gate = [sb("gate%d" % i, [128, TB, 128], BF16) for i in range(NGB)]
    ga = [sb("ga%d" % i, [128, TB * 128], BF16) for i in range(2)]
    PTall = sb("PTall", [128, 2 * SC, TB * 128], BF16)
    PT = [PTall[:, i * SC:(i + 1) * SC, :] for i in range(2)]
    qT_sb = PTall
    psA = ps("psA", [128, 2048], F32)
    psB = ps("psB", [128, 1024], F32)
    psC = ps("psC", [128, 1024], F32)

    b_acc = [Buf("acc%d" % j) for j in range(TB)]
    b_c = Buf("consts")
    b_xt, b_hf, b_hT, b_sk = Buf("xt"), Buf("hf"), Buf("hT"), Buf("sk")
    b_wq = [Buf("wq%d" % i) for i in range(2)]
    b_s, b_s2, b_tk, b_sel, b_selT = Buf("s"), Buf("s2"), Buf("tk"), Buf("sel"), Buf("selT")
    TK = [b_tk, b_s, b_s2, b_xt]
    b_ssq = Buf("ssq")
    b_uT = [Buf("uT%d" % i) for i in range(NUB)]
    b_v = [Buf("v%d" % i) for i in range(NVB)]
    b_gate = [Buf("gate%d" % i) for i in range(NGB)]
    b_ga = [Buf("ga%d" % i) for i in range(2)]
    b_PT = [Buf("PT%d" % i) for i in range(2)]
    b_psA = [Buf("psA%d" % i) for i in range(4)]
    b_psB = [Buf("psB%d" % i) for i in range(2)]
    b_psC = [Buf("psC%d" % i) for i in range(2)]
    b_wg = [Buf("wg%d" % i) for i in range(NT)]
    b_xo = [Buf("xo%d" % i) for i in range(NT)]

    P.dma([('sp', lambda e: e.dma_start(out=g_bc[:], in_=gnorm.partition_broadcast(128)))], 'c0', w=[b_c])
    if final_g is not None:
        P.dma([('sp', lambda e: e.dma_start(out=fg_bc[:], in_=final_g.partition_broadcast(128)))], 'c0', w=[b_c])
    P.dma([('sp', lambda e: e.dma_start(out=ident[:], in_=ident_d))], 'c0', w=[b_c])
    P.dma([('sp', lambda e: e.dma_start(out=iota[:], in_=iota_d))], 'c0', w=[b_c])
    P.op('dve', lambda e: e.tensor_scalar(out=i16[:], in0=iota[:, 0:16], scalar1=16.0, scalar2=None, op0=ALU.mult), r=[b_c], w=[b_c])
    P.dma([('pool', lambda e: e.dma_start(out=sk_bf[:], in_=skT.rearrange("g d k -> d g k")))], 'c1', w=[b_sk])

    wq_n = [0]
    u_n, v_n, g_n, ga_n = [0], [0], [0], [0]
    psb_n, pair_n = [0], [0]

    nblk = (NT + TB - 1) // TB
    conv = None
    if conv_args is not None:
        conv = Converter(nc, P, stack, *conv_args, ncalls=(nblk - 1) * (NE // 128))
    for blk in range(nblk):
        t0 = blk * TB
        nb = min(TB, NT - t0)
        N = nb * 128
        for j in range(nb):
            tt = t0 + j
            P.dma([('sp', lambda e, tt=tt: e.dma_start(out=xt[:], in_=x_in[tt * 128:(tt + 1) * 128, :]))], 'xt', w=[b_xt])
            P.op('act', lambda e: e.activation(out=h_f[:], in_=xt[:], func=AF.Square), r=[b_xt], w=[b_hf])
            P.op('dve', lambda e: e.tensor_reduce(out=ssq[:, 0:1], in_=h_f[:], axis=AX.X, op=ALU.add), r=[b_hf], w=[b_ssq])
            P.op('dve', lambda e: e.tensor_scalar(out=ssq[:, 1:2], in0=ssq[:, 0:1], scalar1=1.0 / D, scalar2=EPS,
                                                  op0=ALU.mult, op1=ALU.add), r=[b_ssq], w=[b_ssq])
            P.op('act', lambda e: e.activation(out=ssq[:, 2:3], in_=ssq[:, 1:2], func=AF.Sqrt), r=[b_ssq], w=[b_ssq])
            P.op('dve', lambda e: e.reciprocal(out=ssq[:, 3:4], in_=ssq[:, 2:3]), r=[b_ssq], w=[b_ssq])
            P.op('dve', lambda e: e.scalar_tensor_tensor(out=h_f[:], in0=xt[:], scalar=ssq[:, 3:4],
                                                         in1=g_bc[:], op0=ALU.mult, op1=ALU.mult),
                 r=[b_xt, b_ssq, b_c], w=[b_hf])
            for dc in range(16):
                P.op('pe', lambda e, dc=dc: e.transpose(psA[:, dc * 128:(dc + 1) * 128], h_f[:, dc * 128:(dc + 1) * 128], ident[:]),
                     r=[b_hf, b_c], w=[b_psA[dc // 4]])
            P.op('act', lambda e, j=j: e.activation(out=hT[:, :, j * 128:(j + 1) * 128],
                                                     in_=psA[:].rearrange("p (a b) -> p a b", a=16), func=AF.Copy),
                 r=b_psA, w=[b_hT])
        for hp in range(16):
            wi = wq_n[0] % 2
            wq_n[0] += 1
            P.dma([('sp', lambda e, hp=hp, wi=wi: e.dma_start(out=wq_bf[wi][:], in_=wq[hp].rearrange("p (a b) -> p a b", b=128)))],
                  'wq%d' % wi, w=[b_wq[wi]])
            bi = psb_n[0] % 2
            psb_n[0] += 1
            for dc in range(16):
                P.op('pe', lambda e, dc=dc, wi=wi, bi=bi, N=N: e.matmul(psB[:, bi * 512:bi * 512 + N], lhsT=wq_bf[wi][:, dc, :],
                                                                         rhs=hT[:, dc, 0:N], start=(dc == 0), stop=(dc == 15)),
                     r=[b_wq[wi], b_hT], w=[b_psB[bi]])
            P.op('act', lambda e, hp=hp, bi=bi, N=N: e.activation(out=qT_sb[:, hp, 0:N], in_=psB[:, bi * 512:bi * 512 + N], func=AF.Copy),
                 r=[b_psB[bi]], w=b_PT)
        def stageA(j):
            tt = t0 + j
            for hp in range(16):
                P.op('pe', lambda e, hp=hp, j=j: e.matmul(psA[:, hp * 128:(hp + 1) * 128], lhsT=qT_sb[:, hp, j * 128:(j + 1) * 128],
                                                          rhs=sk_bf[:, hp, :], start=True, stop=True),
                     r=b_PT + [b_sk], w=[b_psA[hp // 4]])
            P.op('act', lambda e: e.activation(out=s_sb[:].rearrange("p a b -> p (a b)"), in_=psA[:], func=AF.Copy),
                 r=b_psA, w=[b_s])
        def stageB(j):
            tt = t0 + j
            for g in range(16):
                P.op('dve', lambda e, g=g: e.max(out=tv[:, g, 0:8], in_=s_sb[:, g, :]), r=[b_s], w=[b_tk])
                P.op('dve', lambda e, g=g: e.max_index(out=ti[:, g, 0:8], in_max=tv[:, g, 0:8], in_values=s_sb[:, g, :]),
                     r=[b_s, b_tk], w=[b_tk])
                P.op('dve', lambda e, g=g: e.match_replace(out=s2_sb[:, g, :], in_to_replace=tv[:, g, 0:8],
                                                           in_values=s_sb[:, g, :], imm_value=-1e30), r=[b_s, b_tk], w=[b_s2])
                P.op('dve', lambda e, g=g: e.max(out=tv[:, g, 8:16], in_=s2_sb[:, g, :]), r=[b_s2], w=[b_tk])
                P.op('dve', lambda e, g=g: e.max_index(out=ti[:, g, 8:16], in_max=tv[:, g, 8:16], in_values=s2_sb[:, g, :]),
                     r=[b_s2, b_tk], w=[b_tk])
            P.op('dve', lambda e: e.tensor_copy(out=tif[:], in_=ti[:]), r=[b_tk], w=[b_tk])
            P.op('dve', lambda e: e.tensor_tensor(out=cand[:].rearrange("p h (j k) -> p h j k", j=16),
                                                  in0=mkap(tv, 0, [[32, 8], [1, 16], [0, 16]]),
                                                  in1=mkap(tv, 16, [[32, 8], [0, 16], [1, 16]]), op=ALU.add),
                 r=TK, w=TK)
            for h in range(8):
                P.op('dve', lambda e, h=h: e.max(out=cv[:, h, 0:8], in_=cand[:, h, :]), r=TK, w=TK)
                P.op('dve', lambda e, h=h: e.max_index(out=ci[:, h, 0:8], in_max=cv[:, h, 0:8], in_values=cand[:, h, :]),
                     r=TK, w=TK)
                P.op('dve', lambda e, h=h: e.match_replace(out=cand2[:, h, :], in_to_replace=cv[:, h, 0:8],
                                                           in_values=cand[:, h, :], imm_value=-1e30), r=TK, w=TK)
                P.op('dve', lambda e, h=h: e.max(out=cv[:, h, 8:16], in_=cand2[:, h, :]), r=TK, w=TK)
                P.op('dve', lambda e, h=h: e.max_index(out=ci[:, h, 8:16], in_max=cv[:, h, 8:16], in_values=cand2[:, h, :]),
                     r=TK, w=TK)
            P.op('dve', lambda e: e.tensor_copy(out=cif[:], in_=ci[:]), r=TK, w=TK)
            P.op('dve', lambda e: e.tensor_tensor(out=oh[:], in0=mkap(cif, 0, [[1, 128], [0, 16]]),
                                                  in1=mkap(i16, 0, [[0, 128], [1, 16]]), op=ALU.is_ge), r=TK + [b_c], w=TK)
            P.op('dve', lambda e: e.tensor_reduce(out=jf[:].rearrange("p h c -> p (h c)"), in_=oh[:], axis=AX.X, op=ALU.add),
                 r=TK, w=TK)
            P.op('dve', lambda e: e.tensor_scalar(out=jf[:], in0=jf[:], scalar1=-1.0, scalar2=None, op0=ALU.add), r=TK, w=TK)
            P.op('dve', lambda e: e.scalar_tensor_tensor(out=kf[:].rearrange("p h c -> p (h c)"), in0=jf[:].rearrange("p h c -> p (h c)"),
                                                         scalar=-16.0, in1=cif[:].rearrange("p h c -> p (h c)"), op0=ALU.mult, op1=ALU.add),
                 r=TK, w=TK)
            for which, src, off in ((0, jf, 0), (1, kf, 16)):
                P.op('dve', lambda e, src=src: e.tensor_tensor(out=oh[:], in0=mkap(src, 0, [[1, 128], [0, 16]]),
                                                               in1=mkap(iota, 0, [[0, 128], [1, 16]]), op=ALU.is_equal),
                     r=TK + [b_c], w=TK)
                P.op('dve', lambda e, off=off: e.tensor_tensor(out=oh[:].rearrange("p (h c) j -> p h c j", h=8),
                                                               in0=oh[:].rearrange("p (h c) j -> p h c j", h=8),
                                                               in1=mkap(tif, off, [[32, 8], [0, 16], [1, 16]]), op=ALU.mult),
                     r=TK, w=TK)
                P.op('dve', lambda e, which=which: e.tensor_reduce(out=sel[:, which, :], in_=oh[:], axis=AX.X, op=ALU.add),
                     r=TK + [b_selT], w=TK + [b_sel])
            P.op('dve', lambda e: e.tensor_tensor(out=jf[:], in0=cv[:], in1=mkap(cv, 0, [[16, 8], [0, 16]]), op=ALU.subtract),
                 r=TK, w=TK)
            P.op('act', lambda e: e.activation(out=kf[:], in_=jf[:], func=AF.Exp), r=TK, w=TK)
            P.op('dve', lambda e: e.tensor_reduce(out=zz[:, 0:8], in_=kf[:], axis=AX.X, op=ALU.add), r=TK, w=TK)
            P.op('dve', lambda e: e.reciprocal(out=zz[:, 8:16], in_=zz[:, 0:8]), r=TK, w=TK)
            P.op('dve', lambda e: e.tensor_tensor(out=sel[:, 2, :].rearrange("p (h c) -> p h c", h=8), in0=kf[:],
                                                  in1=mkap(zz, 8, [[1, 8], [0, 16]]), op=ALU.mult),
                 r=TK, w=TK + [b_sel])
            for k3 in range(3):
                P.op('pe', lambda e, k3=k3: e.transpose(psC[:, k3 * 128:(k3 + 1) * 128], sel[:, k3, :], ident[:]),
                     r=[b_sel, b_c], w=[b_psC[0]])
            P.op('act', lambda e: e.activation(out=selT[:].rearrange("p a b -> p (a b)"), in_=psC[:, 0:384], func=AF.Copy),
                 r=[b_psC[0]], w=[b_selT])
        def stageC(j):
            tt = t0 + j
            for qd in range(4):
                ab = qd % 2
                P.op('dve', lambda e, ab=ab, qd=qd: e.tensor_tensor(out=A_q[ab][:], in0=mkap(iota, 0, [[0, 32], [1, 128]]),
                                                                      in1=mkap(selT, qd * 32, [[1, 32], [0, 128]]), op=ALU.is_equal),
                     r=[b_selT, b_c], w=[b_v[2 * ab], b_v[2 * ab + 1]])
                P.op('dve', lambda e, ab=ab, qd=qd: e.tensor_tensor(out=B_q[ab][:], in0=mkap(iota, 0, [[0, 32], [1, 128]]),
                                                                     in1=mkap(selT, 128 + qd * 32, [[1, 32], [0, 128]]), op=ALU.is_equal),
                     r=[b_selT, b_c], w=[b_v[4 + 2 * ab], b_v[5 + 2 * ab]])
                P.op('dve', lambda e, ab=ab, qd=qd: e.tensor_tensor(out=B_q[ab][:], in0=B_q[ab][:],
                                                                     in1=mkap(selT, 256 + qd * 32, [[1, 32], [0, 128]]), op=ALU.mult),
                     r=[b_selT], w=[b_v[4 + 2 * ab], b_v[5 + 2 * ab]])
                for t8 in range(4):
                    for tq in range(8):
                        tl = t8 * 8 + tq
                        P.op('pe', lambda e, ab=ab, tl=tl, tq=tq: e.matmul(psB[:, tq * 128:(tq + 1) * 128], lhsT=B_q[ab][:, tl, :],
                                                                            rhs=A_q[ab][:, tl, :], start=True, stop=True),
                             r=[b_v[2 * ab], b_v[2 * ab + 1], b_v[4 + 2 * ab], b_v[5 + 2 * ab]], w=[b_psB[tq // 4]])
                    tb = qd * 32 + t8 * 8
                    P.op('act', lambda e, tb=tb: e.activation(out=Wst[:, :, tb:tb + 8],
                                                               in_=psB[:].rearrange("p (t i) -> p i t", t=8), func=AF.Copy),
                         r=b_psB, w=b_acc)
            P.dma([('sp', lambda e, tt=tt: e.dma_start(out=wg[tt].rearrange("p i t -> p (i t)"),
                                                       in_=Wst.rearrange("p i t -> p (i t)")))],
                  'wst', r=b_acc, w=[b_wg[tt]])
        stageA(0)
        stageB(0)
        for j in range(nb):
            if j + 1 < nb:
                stageA(j + 1)
            stageC(j)
            if j + 1 < nb:
                stageB(j + 1)
        for j in range(nb):
            P.dma([('sp', lambda e, j=j, tt=t0 + j: e.dma_start(out=acc[:, j, :], in_=x_in[tt * 128:(tt + 1) * 128, :]))],
                  'x%d' % j, w=[b_acc[j]])
        for sc in range(NE // 128 // SC):
            pt = sc % 2
            vslots = []
            for cix in range(SC):
                i1 = sc * SC + cix
                ui = u_n[0] % NUB
                u_n[0] += 1
                vi = v_n[0] % NVB
                v_n[0] += 1
                gi = g_n[0] % NGB
                g_n[0] += 1
                gai = ga_n[0] % 2
                ga_n[0] += 1
                bi = psb_n[0] % 2
                psb_n[0] += 1
                vslots.append(vi)
                if conv is not None:
                    conv.tick()
                P.dma([('sp', lambda e, ui=ui, i1=i1: e.dma_start(out=uT_bf[ui][:].rearrange("p a b -> p (a b)"), in_=uT[i1]))],
                      'u%d' % ui, w=[b_uT[ui]])
                P.dma([('sp', lambda e, vi=vi, i1=i1: e.dma_start(out=v_bf[vi][:], in_=v[i1 * 128:(i1 + 1) * 128, :]))],
                      'v%d' % vi, w=[b_v[vi]])
                P.dma([('sp', lambda e, gi=gi, i1=i1, t0=t0, nb=nb: e.dma_start(
                    out=gate[gi][:, 0:nb, :], in_=wg[t0:t0 + nb, :, i1, :].rearrange("n p t -> p n t")))],
                    'g%d' % gi, r=[b_wg[t0 + jj] for jj in range(nb)], w=[b_gate[gi]])
                for dc in range(16):
                    P.op('pe', lambda e, dc=dc, ui=ui, bi=bi, N=N: e.matmul(psB[:, bi * 512:bi * 512 + N], lhsT=uT_bf[ui][:, dc, :],
                                                                             rhs=hT[:, dc, 0:N], start=(dc == 0), stop=(dc == 15)),
                         r=[b_uT[ui], b_hT], w=[b_psB[bi]])
                P.op('act', lambda e, gai=gai, bi=bi, N=N: e.activation(out=ga[gai][:, 0:N], in_=psB[:, bi * 512:bi * 512 + N],
                                                                         func=AF.Gelu_apprx_tanh),
                     r=[b_psB[bi]], w=[b_ga[gai]])
                P.op('dve', lambda e, gai=gai, gi=gi, pt=pt, cix=cix, N=N: e.tensor_tensor(
                    out=PT[pt][:, cix, 0:N], in0=ga[gai][:, 0:N], in1=gate[gi][:].rearrange("p a b -> p (a b)")[:, 0:N], op=ALU.mult),
                    r=[b_ga[gai], b_gate[gi]], w=[b_PT[pt]])
            for j in range(nb):
                for half in range(2):
                    pr = pair_n[0] % 3
                    pair_n[0] += 1
                    if pr < 2:
                        pst, pbufs, pbase = psA, [b_psA[2 * pr], b_psA[2 * pr + 1]], pr * 1024
                    else:
                        pst, pbufs, pbase = psC, b_psC, 0
                    for cix in range(SC):
                        for nq in range(2):
                            col = half * 1024 + nq * 512
                            P.op('pe', lambda e, pst=pst, pbase=pbase, nq=nq, pt=pt, cix=cix, j=j, col=col, vi=vslots[cix]: e.matmul(
                                pst[:, pbase + nq * 512:pbase + (nq + 1) * 512], lhsT=PT[pt][:, cix, j * 128:(j + 1) * 128],
                                rhs=v_bf[vi][:, col:col + 512], start=(cix == 0), stop=(cix == SC - 1)),
                                r=[b_PT[pt], b_v[vslots[cix]]], w=[pbufs[nq]])
                    P.op('dve', lambda e, pst=pst, pbase=pbase, j=j, half=half: e.tensor_tensor(
                        out=acc[:, j, half * 1024:(half + 1) * 1024], in0=pst[:, pbase:pbase + 1024],
                        in1=acc[:, j, half * 1024:(half + 1) * 1024], op=ALU.add),
                        r=pbufs + [b_acc[j]], w=[b_acc[j]])
        for j in range(nb):
            tt = t0 + j
            if final_g is not None:
                P.op('act', lambda e, j=j: e.activation(out=h_f[:], in_=acc[:, j, :], func=AF.Square), r=[b_acc[j]], w=[b_hf])
                P.op('dve', lambda e: e.tensor_reduce(out=ssq[:, 4:5], in_=h_f[:], axis=AX.X, op=ALU.add), r=[b_hf], w=[b_ssq])
                P.op('dve', lambda e: e.tensor_scalar(out=ssq[:, 5:6], in0=ssq[:, 4:5], scalar1=1.0 / D, scalar2=EPS,
                                                      op0=ALU.mult, op1=ALU.add), r=[b_ssq], w=[b_ssq])
                P.op('act', lambda e: e.activation(out=ssq[:, 6:7], in_=ssq[:, 5:6], func=AF.Sqrt), r=[b_ssq], w=[b_ssq])
                P.op('dve', lambda e: e.reciprocal(out=ssq[:, 7:8], in_=ssq[:, 6:7]), r=[b_ssq], w=[b_ssq])
                P.op('dve', lambda e, j=j: e.scalar_tensor_tensor(out=acc[:, j, :], in0=acc[:, j, :], scalar=ssq[:, 7:8],
                                                                  in1=fg_bc[:], op0=ALU.mult, op1=ALU.mult),
                     r=[b_acc[j], b_ssq, b_c], w=[b_acc[j]])
            P.dma([('sp', lambda e, j=j, tt=tt: e.dma_start(out=x_out[tt * 128:(tt + 1) * 128, :], in_=acc[:, j, :]))],
                  'xo%d' % j, r=[b_acc[j]], w=[b_xo[tt]])
    if conv is not None:
        conv.finish()


def fourier_phase(nc, P, stack, NT, xseq, xown, x_out, gnorm, ccsc, cs, wo, ident_d, conv_args=None):
    pfx = "fr%d_" % P.cnt['pe']
    sb = lambda name, shape, dt: stack.enter_context(nc.sbuf_tensor(pfx + name, shape, dt))
    ps = lambda name, shape, dt: stack.enter_context(nc.psum_tensor(pfx + name, shape, dt))
    NS = 32
    NSL = 6
    g_bc = sb("g_bc", [128, D], F32)
    ident = sb("ident", [128, 128], F32)
    xfull = sb("xfull", [128, D], F32)
    junk = sb("junk", [128, D], F32)
    ssq = sb("ssq", [128, NS], F32)
    rstd = sb("rstd", [128, NS], F32)
    cc_bf = sb("cc_bf", [128, 2, 512], BF16)
    xs = [sb("xs%d" % i, [128, 512], F32) for i in range(2)]
    h_f = sb("h_f", [128, 512], F32)
    hTt = sb("hTt", [128, 4, 128], BF16)
    G_sb = sb("G_sb", [128, 2, NS, 512], BF16)
    slab = [sb("slab%d" % i, [128, 2, 512], BF16) for i in range(NSL)]
    yT = sb("yT", [128, 4, NT * 128], BF16)
    wo_bf = sb("wo_bf", [128, 4, D], BF16)
    xr = [sb("xr%d" % i, [128, D], F32) for i in range(2)]
    psA = ps("psA", [128, 2048], F32)
    psB = ps("psB", [128, 1024], F32)
    psC = ps("psC", [128, 512], F32)
    b_c, b_xfull, b_junk, b_ssq, b_cc = Buf("c"), Buf("xfull"), Buf("junk"), Buf("ssq"), Buf("cc")
    b_xs = [Buf("xs%d" % i) for i in range(2)]
    b_hf, b_hTt, b_G, b_yT, b_wo = Buf("hf"), Buf("hTt"), Buf("G"), Buf("yT"), Buf("wo")
    b_slab = [Buf("slab%d" % i) for i in range(NSL)]
    b_xr = [Buf("xr%d" % i) for i in range(2)]
    b_psA = [Buf("psA%d" % i) for i in range(4)]
    b_psB, b_psC = Buf("psB"), Buf("psC")
    b_x1 = [Buf("x1_%d" % i) for i in range(NT)]

    P.dma([('sp', lambda e: e.dma_start(out=g_bc[:], in_=gnorm.partition_broadcast(128)))], 'c0', w=[b_c])
    P.dma([('sp', lambda e: e.dma_start(out=ident[:], in_=ident_d))], 'c0', w=[b_c])
    P.dma([('sp', lambda e: e.dma_start(out=cc_bf[:], in_=ccsc.rearrange("(k p) c -> p k c", p=128)))], 'c1', w=[b_cc])
    for st in range(NS):
        P.dma([('sp', lambda e, st=st: e.dma_start(out=xfull[:], in_=xseq[st * 128:(st + 1) * 128, :]))], 'xf', w=[b_xfull])
        P.op('act', lambda e: e.activation(out=junk[:], in_=xfull[:], func=AF.Square), r=[b_xfull], w=[b_junk])
        P.op('dve', lambda e, st=st: e.tensor_reduce(out=ssq[:, st:st + 1], in_=junk[:], axis=AX.X, op=ALU.add), r=[b_junk], w=[b_ssq])
    P.op('dve', lambda e: e.tensor_scalar(out=ssq[:], in0=ssq[:], scalar1=1.0 / D, scalar2=EPS, op0=ALU.mult, op1=ALU.add),
         r=[b_ssq], w=[b_ssq])
    P.op('act', lambda e: e.activation(out=ssq[:], in_=ssq[:], func=AF.Sqrt), r=[b_ssq], w=[b_ssq])
    P.op('dve', lambda e: e.reciprocal(out=rstd[:], in_=ssq[:]), r=[b_ssq], w=[b_ssq])
    conv = None
    if conv_args is not None:
        conv = Converter(nc, P, stack, *conv_args, ncalls=4 * NS + 4 * NS * 5 - 40)
    sblocks = []
    s0 = 0
    while s0 < NT * 128:
        n = min(512, NT * 128 - s0)
        sblocks.append((s0, n))
        s0 += n
    xs_n, sl_n, xr_n = [0], [0], [0]
    for cp in range(4):
        c0 = cp * 512
        P.dma([('pool', lambda e, c0=c0: e.dma_start(out=wo_bf[:], in_=wo[c0:c0 + 512, :].rearrange("(k p) c -> p k c", p=128)))],
              'wo', w=[b_wo])
        for st in range(NS):
            xi = xs_n[0] % 2
            xs_n[0] += 1
            P.dma([('sp', lambda e, st=st, xi=xi, c0=c0: e.dma_start(out=xs[xi][:], in_=xseq[st * 128:(st + 1) * 128, c0:c0 + 512]))],
                  'xs%d' % xi, w=[b_xs[xi]])
            P.op('dve', lambda e, st=st, xi=xi, c0=c0: e.scalar_tensor_tensor(out=h_f[:], in0=xs[xi][:], scalar=rstd[:, st:st + 1],
                                                                             in1=g_bc[:, c0:c0 + 512], op0=ALU.mult, op1=ALU.mult),
                 r=[b_xs[xi], b_ssq, b_c], w=[b_hf])
            for k in range(4):
                P.op('pe', lambda e, k=k: e.transpose(psC[:, k * 128:(k + 1) * 128], h_f[:, k * 128:(k + 1) * 128], ident[:]),
                     r=[b_hf, b_c], w=[b_psC])
            P.op('act', lambda e: e.activation(out=hTt[:].rearrange("p a b -> p (a b)"), in_=psC[:], func=AF.Copy), r=[b_psC], w=[b_hTt])
            for gg in range(2):
                for which in range(2):
                    for kc in range(2):
                        P.op('pe', lambda e, gg=gg, which=which, kc=kc: e.matmul(
                            psB[:, (gg * 2 + which) * 256:(gg * 2 + which + 1) * 256], lhsT=hTt[:, gg * 2 + kc, :],
                            rhs=cc_bf[:, kc, which * 256:(which + 1) * 256], start=(kc == 0), stop=(kc == 1)),
                            r=[b_hTt, b_cc], w=[b_psB])
            if conv is not None:
                conv.tick()
            for which in range(2):
                P.op('act', lambda e, st=st, which=which: e.activation(
                    out=G_sb[:, which, st, :].rearrange("p (g c) -> p g c", g=2),
                    in_=psB[:].rearrange("p (g w c) -> p g w c", g=2, w=2)[:, :, which, :], func=AF.Copy),
                    r=[b_psB], w=[b_G])
        for (s0, n) in sblocks:
            for st in range(NS):
                si = sl_n[0] % NSL
                sl_n[0] += 1
                P.dma([('sp', lambda e, st=st, si=si, s0=s0, n=n: e.dma_start(out=slab[si][:, :, 0:n],
                                                                              in_=cs[st * 128:(st + 1) * 128, :, s0:s0 + n]))],
                      'sl%d' % si, w=[b_slab[si]])
                if conv is not None:
                    conv.tick()
                for dq in range(4):
                    for which in range(2):
                        P.op('pe', lambda e, st=st, si=si, dq=dq, which=which, n=n: e.matmul(
                            psA[:, dq * 512:dq * 512 + n], lhsT=G_sb[:, which, st, dq * 128:(dq + 1) * 128],
                            rhs=slab[si][:, which, 0:n], start=(st == 0 and which == 0), stop=(st == NS - 1 and which == 1)),
                            r=[b_G, b_slab[si]], w=[b_psA[dq]])
            for dq in range(4):
                P.op('act', lambda e, dq=dq, s0=s0, n=n: e.activation(out=yT[:, dq, s0:s0 + n], in_=psA[:, dq * 512:dq * 512 + n], func=AF.Copy),
                     r=[b_psA[dq]], w=[b_yT])
        for tt in range(NT):
            for nq in range(4):
                for kc in range(4):
                    P.op('pe', lambda e, tt=tt, nq=nq, kc=kc: e.matmul(psA[:, nq * 512:(nq + 1) * 512], lhsT=yT[:, kc, tt * 128:(tt + 1) * 128],
                                                                        rhs=wo_bf[:, kc, nq * 512:(nq + 1) * 512], start=(kc == 0), stop=(kc == 3)),
                         r=[b_yT, b_wo], w=[b_psA[nq]])
            ri = xr_n[0] % 2
            xr_n[0] += 1
            src = xown if cp == 0 else x_out
            P.dma([('sp', lambda e, tt=tt, ri=ri, src=src: e.dma_start(out=xr[ri][:], in_=src[tt * 128:(tt + 1) * 128, :]))],
                  'xr%d' % ri, r=[b_x1[tt]], w=[b_xr[ri]])
            P.op('dve', lambda e, ri=ri: e.tensor_tensor(out=xr[ri][:], in0=psA[:], in1=xr[ri][:], op=ALU.add),
                 r=b_psA + [b_xr[ri]], w=[b_xr[ri]])
            P.dma([('sp', lambda e, tt=tt, ri=ri: e.dma_start(out=x_out[tt * 128:(tt + 1) * 128, :], in_=xr[ri][:]))],
                  'xw%d' % ri, r=[b_xr[ri]], w=[b_x1[tt]])
    if conv is not None:
        conv.finish()


def attn_phase(nc, P, semstack, x_in, x_out, gnorm, wqk, wqk_sw, wv, wo, sinks, cos_t, sin_t, amask, ident_d, conv_args=None):
    from contextlib import ExitStack
    NTK, NTQ = 17, 16
    pfx = "at%d_" % P.cnt['pe']
    with ExitStack() as outer:
        sbo = lambda name, shape, dt: outer.enter_context(nc.sbuf_tensor(pfx + name, shape, dt))
        hT_all = sbo("hT_all", [128, 16, NTK * 128], BF16)
        oT_all = hT_all
        qT_all = sbo("qT_all", [128, 16, NTQ * 128], BF16)
        kT_all = sbo("kT_all", [128, 4, NTK * 128], BF16)
        v_all = sbo("v_all", [128, NTK, 512], BF16)
        ident = sbo("ident", [128, 128], F32)
        ident_bf = sbo("ident_bf", [128, 128], BF16)
        b_hT, b_q, b_k, b_v, b_c = Buf("hT"), Buf("q"), Buf("k"), Buf("v"), Buf("c")
        with ExitStack() as st1:
            sb = lambda name, shape, dt: st1.enter_context(nc.sbuf_tensor(pfx + "a1" + name, shape, dt))
            g_bc = sb("g_bc", [128, D], F32)
            xt = [sb("xt%d" % i, [128, D], F32) for i in range(2)]
            h_f = sb("h_f", [128, D], F32)
            ssq = sb("ssq", [128, 4], F32)
            psA = st1.enter_context(nc.psum_tensor(pfx + "a1psA", [128, 2048], F32))
            b_xt = [Buf("xt0"), Buf("xt1")]
            b_hf, b_ssq = Buf("hf"), Buf("ssq")
            b_psA = [Buf("psA%d" % i) for i in range(4)]
            P.dma([('sp', lambda e: e.dma_start(out=g_bc[:], in_=gnorm.partition_broadcast(128)))], 'c0', w=[b_c])
            P.dma([('sp', lambda e: e.dma_start(out=ident[:], in_=ident_d))], 'c0', w=[b_c])
            P.op('dve', lambda e: e.tensor_copy(out=ident_bf[:], in_=ident[:]), r=[b_c], w=[b_c])
            for tt in range(NTK):
                xi = tt % 2
                P.dma([('sp', lambda e, tt=tt, xi=xi: e.dma_start(out=xt[xi][:], in_=x_in[tt * 128:(tt + 1) * 128, :]))],
                      'xt%d' % xi, w=[b_xt[xi]])
                P.op('act', lambda e, xi=xi: e.activation(out=h_f[:], in_=xt[xi][:], func=AF.Square), r=[b_xt[xi]], w=[b_hf])
                P.op('dve', lambda e: e.tensor_reduce(out=ssq[:, 0:1], in_=h_f[:], axis=AX.X, op=ALU.add), r=[b_hf], w=[b_ssq])
                P.op('dve', lambda e: e.tensor_scalar(out=ssq[:, 1:2], in0=ssq[:, 0:1], scalar1=1.0 / D, scalar2=EPS,
                                                      op0=ALU.mult, op1=ALU.add), r=[b_ssq], w=[b_ssq])
                P.op('act', lambda e: e.activation(out=ssq[:, 2:3], in_=ssq[:, 1:2], func=AF.Sqrt), r=[b_ssq], w=[b_ssq])
                P.op('dve', lambda e: e.reciprocal(out=ssq[:, 3:4], in_=ssq[:, 2:3]), r=[b_ssq], w=[b_ssq])
                P.op('dve', lambda e, xi=xi: e.scalar_tensor_tensor(out=h_f[:], in0=xt[xi][:], scalar=ssq[:, 3:4], in1=g_bc[:],
                                                                    op0=ALU.mult, op1=ALU.mult), r=[b_xt[xi], b_ssq, b_c], w=[b_hf])
                for dc in range(16):
                    P.op('pe', lambda e, dc=dc: e.transpose(psA[:, dc * 128:(dc + 1) * 128], h_f[:, dc * 128:(dc + 1) * 128], ident[:]),
                         r=[b_hf, b_c], w=[b_psA[dc // 4]])
                P.op('act', lambda e, tt=tt: e.activation(out=hT_all[:, :, tt * 128:(tt + 1) * 128],
                                                           in_=psA[:].rearrange("p (a b) -> p a b", a=16), func=AF.Copy),
                     r=b_psA, w=[b_hT])
            P.flush(semstack)
        with ExitStack() as st2:
            sb = lambda name, shape, dt: st2.enter_context(nc.sbuf_tensor(pfx + "a2" + name, shape, dt))
            cosb = sb("cosb", [128, NTK * 128], F32)
            sinb = sb("sinb", [128, NTK * 128], F32)
            wch = [sb("wch%d" % i, [128, 16, 128], BF16) for i in range(2)]
            wsw = [sb("wsw%d" % i, [128, 16, 128], BF16) for i in range(2)]
            tmp1 = sb("tmp1", [128, 512], F32)
            tmp2 = sb("tmp2", [128, 512], F32)
            psX = st2.enter_context(nc.psum_tensor(pfx + "a2psX", [128, 1024], F32))
            psY = st2.enter_context(nc.psum_tensor(pfx + "a2psY", [128, 1024], F32))
            psV = st2.enter_context(nc.psum_tensor(pfx + "a2psV", [128, 1024], F32))
            b_tab, b_wv, b_t1, b_t2 = Buf("tab"), Buf("wv"), Buf("t1"), Buf("t2")
            b_w = [Buf("w0"), Buf("w1")]
            b_psX, b_psY, b_psV = [Buf("x0"), Buf("x1")], [Buf("y0"), Buf("y1")], [Buf("v0"), Buf("v1")]
            P.dma([('sp', lambda e: e.dma_start(out=cosb[:], in_=cos_t))], 'c0', w=[b_tab])
            P.dma([('sp', lambda e: e.dma_start(out=sinb[:], in_=sin_t))], 'c0', w=[b_tab])
            wr = wqk.rearrange("(k p) c -> p k c", p=128)
            wsr = wqk_sw.rearrange("(k p) c -> p k c", p=128)
            pn = [0]
            for ch in list(range(16, 20)) + list(range(16)):
                wi = ch % 2
                P.dma([('pool', lambda e, ch=ch, wi=wi: e.dma_start(out=wch[wi][:], in_=wr[:, :, ch * 128:(ch + 1) * 128]))],
                      'wa%d' % wi, w=[b_w[wi]])
                P.dma([('pool', lambda e, ch=ch, wi=wi: e.dma_start(out=wsw[wi][:], in_=wsr[:, :, ch * 128:(ch + 1) * 128]))],
                      'wb%d' % wi, w=[b_w[wi]])
                ntok = (NTK if ch >= 16 else NTQ) * 128
                g0 = 0
                while g0 < ntok:
                    n = min(512, ntok - g0)
                    pi = pn[0] % 2
                    pn[0] += 1
                    for dc in range(16):
                        P.op('pe', lambda e, dc=dc, wi=wi, pi=pi, g0=g0, n=n: e.matmul(psX[:, pi * 512:pi * 512 + n], lhsT=wch[wi][:, dc, :],
                                                                                      rhs=hT_all[:, dc, g0:g0 + n], start=(dc == 0), stop=(dc == 15)),
                             r=[b_w[wi], b_hT], w=[b_psX[pi]])
                    for dc in range(16):
                        P.op('pe', lambda e, dc=dc, wi=wi, pi=pi, g0=g0, n=n: e.matmul(psY[:, pi * 512:pi * 512 + n], lhsT=wsw[wi][:, dc, :],
                                                                                      rhs=hT_all[:, dc, g0:g0 + n], start=(dc == 0), stop=(dc == 15)),
                             r=[b_w[wi], b_hT], w=[b_psY[pi]])
                    P.op('dve', lambda e, pi=pi, g0=g0, n=n: e.tensor_tensor(out=tmp1[:, 0:n], in0=psX[:, pi * 512:pi * 512 + n],
                                                                             in1=cosb[:, g0:g0 + n], op=ALU.mult),
                         r=[b_psX[pi], b_tab], w=[b_t1])
                    P.op('dve', lambda e, pi=pi, g0=g0, n=n: e.tensor_tensor(out=tmp2[:, 0:n], in0=psY[:, pi * 512:pi * 512 + n],
                                                                             in1=sinb[:, g0:g0 + n], op=ALU.mult),
                         r=[b_psY[pi], b_tab], w=[b_t2])
                    if ch >= 16:
                        P.op('dve', lambda e, ch=ch, g0=g0, n=n: e.tensor_tensor(out=kT_all[:, ch - 16, g0:g0 + n], in0=tmp1[:, 0:n],
                                                                                 in1=tmp2[:, 0:n], op=ALU.add), r=[b_t1, b_t2], w=[b_k])
                    else:
                        P.op('dve', lambda e, ch=ch, g0=g0, n=n: e.tensor_tensor(out=qT_all[:, ch, g0:g0 + n], in0=tmp1[:, 0:n],
                                                                                 in1=tmp2[:, 0:n], op=ALU.add), r=[b_t1, b_t2], w=[b_q])
                    g0 += n
            wvr = wv.rearrange("(k p) c -> p k c", p=128)
            for vc in range(4):
                wi = vc % 2
                P.dma([('pool', lambda e, vc=vc, wi=wi: e.dma_start(out=wch[wi][:], in_=wvr[:, :, vc * 128:(vc + 1) * 128]))],
                      'wa%d' % wi, w=[b_w[wi]])
                for tt in range(NTK):
                    pi = tt % 2
                    for dc in range(16):
                        P.op('pe', lambda e, dc=dc, tt=tt, pi=pi, wi=wi: e.matmul(psV[:, pi * 512:pi * 512 + 128], lhsT=hT_all[:, dc, tt * 128:(tt + 1) * 128],
                                                                                   rhs=wch[wi][:, dc, :], start=(dc == 0), stop=(dc == 15)),
                             r=[b_hT, b_w[wi]], w=[b_psV[pi]])
                    P.op('act', lambda e, tt=tt, pi=pi, vc=vc: e.activation(out=v_all[:, tt, vc * 128:(vc + 1) * 128], in_=psV[:, pi * 512:pi * 512 + 128], func=AF.Copy),
                         r=[b_psV[pi]], w=[b_v])
            P.flush(semstack)
        with ExitStack() as st3:
            sb = lambda name, shape, dt: st3.enter_context(nc.sbuf_tensor(pfx + "a3" + name, shape, dt))
            sink_bc = sb("sink_bc", [128, 32], F32)
            m_sb = [sb("m_sb%d" % i, [128, 384], F32) for i in range(2)]
            sm = [sb("sm0", [128, 4, 384], F32)] * 2
            p_bf = [sb("p_bf0", [128, 4, 384], BF16)] * 2
            pT_sb = [sb("pT_sb0", [128, 4, 384], BF16)] * 2
            stt = [sb("stt%d" % i, [128, 8, 4], F32) for i in range(2)]
            o_blk = [sb("o_blk%d" % i, [128, 512], F32) for i in range(2)]
            s_ps = st3.enter_context(nc.psum_tensor(pfx + "a3s", [128, 4, 512], F32))
            pT_ps = st3.enter_context(nc.psum_tensor(pfx + "a3pT", [128, 4, 512], BF16))
            o_ps = st3.enter_context(nc.psum_tensor(pfx + "a3o", [128, 512], F32))
            psT = st3.enter_context(nc.psum_tensor(pfx + "a3psT", [128, 512], F32))
            b_sink = Buf("sink")
            b_m = [Buf("m0"), Buf("m1")]
            b_sm = [Buf("sm0")] * 2
            b_p = [Buf("p0")] * 2
            b_pT = [Buf("pT0")] * 2
            b_st = [Buf("st%d" % i) for i in range(2)]
            b_ob = [Buf("ob0"), Buf("ob1")]
            b_sps = [Buf("sps%d" % i) for i in range(4)]
            b_pTps = [Buf("pTps0"), Buf("pTps1")]
            b_ops, b_psT = Buf("ops"), Buf("psT")
            P.dma([('sp', lambda e: e.dma_start(out=sink_bc[:], in_=sinks.partition_broadcast(128)))], 'c0', w=[b_sink])
            conv = None
            if conv_args is not None:
                conv = Converter(nc, P, st3, *conv_args, ncalls=60, R=4)
            gn_ = [0]
            for blk in range(NTQ):
                mi = blk % 2
                P.dma([('sp', lambda e, blk=blk, mi=mi: e.dma_start(out=m_sb[mi][:], in_=amask[blk]))], 'mk%d' % mi, w=[b_m[mi]])
                kt = [blk - 1 if blk > 0 else 16, blk, blk + 1 if blk < 15 else 16]
                for p in range(4):
                    if conv is not None:
                        conv.tick()
                    ob = (blk * 4 + p) % 2
                    for par in range(2):
                        r2 = gn_[0] % 2
                        gn_[0] += 1
                        lo, hi = par * 64, par * 64 + 64
                        h0 = 8 * p + 4 * par
                        for c in range(4):
                            for j in range(3):
                                P.op('pe', lambda e, j=j, lo=lo, hi=hi, p=p, c=c, blk=blk, ktj=kt[j]: e.matmul(
                                    s_ps[:, c, j * 128:(j + 1) * 128], lhsT=qT_all[lo:hi, p * 4 + c, blk * 128:(blk + 1) * 128],
                                    rhs=kT_all[lo:hi, p, ktj * 128:(ktj + 1) * 128], start=True, stop=True),
                                    r=[b_q, b_k], w=[b_sps[c]])
                        P.op('dve', lambda e, r2=r2, mi=mi: e.tensor_tensor(out=sm[r2][:], in0=s_ps[:, :, 0:384],
                                                                            in1=mkap(m_sb[mi], 0, [[0, 4], [1, 384]]), op=ALU.add),
                             r=b_sps + [b_m[mi]], w=[b_sm[r2]])
                        P.op('dve', lambda e, r2=r2: e.tensor_reduce(out=stt[r2][:, 0, :], in_=sm[r2][:], axis=AX.X, op=ALU.max),
                             r=[b_sm[r2]], w=[b_st[r2]])
                        P.op('dve', lambda e, r2=r2, h0=h0: e.scalar_tensor_tensor(out=stt[r2][:, 1, :], in0=stt[r2][:, 0, :], scalar=0.125,
                                                                                 in1=sink_bc[:, h0:h0 + 4], op0=ALU.mult, op1=ALU.max),
                             r=[b_st[r2], b_sink], w=[b_st[r2]])
                        P.op('dve', lambda e, r2=r2: e.tensor_scalar(out=stt[r2][:, 2, :], in0=stt[r2][:, 1, :], scalar1=-1.0, scalar2=None,
                                                                     op0=ALU.mult), r=[b_st[r2]], w=[b_st[r2]])
                        P.op('dve', lambda e, r2=r2, h0=h0: e.tensor_tensor(out=stt[r2][:, 3, :], in0=sink_bc[:, h0:h0 + 4], in1=stt[r2][:, 2, :],
                                                                            op=ALU.add), r=[b_st[r2], b_sink], w=[b_st[r2]])
                        for c in range(4):
                            P.op('act', lambda e, r2=r2, c=c: e.activation(out=p_bf[r2][:, c, :], in_=sm[r2][:, c, :], func=AF.Exp,
                                                                           bias=stt[r2][:, 2, c:c + 1], scale=0.125),
                                 r=[b_sm[r2], b_st[r2]], w=[b_p[r2]])
                        P.op('act', lambda e, r2=r2: e.activation(out=stt[r2][:, 4, :], in_=stt[r2][:, 3, :], func=AF.Exp),
                             r=[b_st[r2]], w=[b_st[r2]])
                        P.op('dve', lambda e, r2=r2: e.tensor_reduce(out=stt[r2][:, 5, :], in_=p_bf[r2][:], axis=AX.X, op=ALU.add),
                             r=[b_p[r2]], w=[b_st[r2]])
                        P.op('dve', lambda e, r2=r2: e.tensor_tensor(out=stt[r2][:, 6, :], in0=stt[r2][:, 5, :], in1=stt[r2][:, 4, :], op=ALU.add),
                             r=[b_st[r2]], w=[b_st[r2]])
                        P.op('dve', lambda e, r2=r2: e.reciprocal(out=stt[r2][:, 7, :], in_=stt[r2][:, 6, :]), r=[b_st[r2]], w=[b_st[r2]])
                        for c in range(4):
                            for j in range(3):
                                P.op('pe', lambda e, j=j, c=c, r2=r2: e.transpose(pT_ps[:, c, j * 128:(j + 1) * 128],
                                                                                  p_bf[r2][:, c, j * 128:(j + 1) * 128], ident_bf[:]),
                                     r=[b_p[r2], b_c], w=[b_pTps[c // 2]])
                        P.op('act', lambda e, r2=r2: e.activation(out=pT_sb[r2][:], in_=pT_ps[:, :, 0:384], func=AF.Copy),
                             r=b_pTps, w=[b_pT[r2]])
                        vc = (2 * p + par) * 64
                        for c in range(4):
                            for j in range(3):
                                P.op('pe', lambda e, j=j, c=c, r2=r2, vc=vc, ktj=kt[j]: e.matmul(
                                    o_ps[:, c * 64:(c + 1) * 64], lhsT=pT_sb[r2][:, c, j * 128:(j + 1) * 128], rhs=v_all[:, ktj, vc:vc + 64],
                                    start=(j == 0), stop=(j == 2)), r=[b_pT[r2], b_v], w=[b_ops])
                        P.op('dve', lambda e, r2=r2, ob=ob, par=par: e.tensor_tensor(
                            out=o_blk[ob][:, par * 256:(par + 1) * 256].rearrange("p (c d) -> p c d", c=4),
                            in0=o_ps[:, 0:256].rearrange("p (c d) -> p c d", c=4),
                            in1=mkap(stt[r2], 28, [[1, 4], [0, 64]]), op=ALU.mult),
                            r=[b_ops, b_st[r2]], w=[b_ob[ob]])
                    for k in range(4):
                        P.op('pe', lambda e, k=k, ob=ob: e.transpose(psT[:, k * 128:(k + 1) * 128], o_blk[ob][:, k * 128:(k + 1) * 128], ident[:]),
                             r=[b_ob[ob], b_c], w=[b_psT])
                    P.op('act', lambda e, p=p, blk=blk: e.activation(out=oT_all[:, p * 4:(p + 1) * 4, blk * 128:(blk + 1) * 128],
                                                                      in_=psT[:].rearrange("p (a b) -> p a b", a=4), func=AF.Copy),
                         r=[b_psT], w=[b_hT])
            if conv is not None:
                conv.finish()
            P.flush(semstack)
        with ExitStack() as st4:
            sb = lambda name, shape, dt: st4.enter_context(nc.sbuf_tensor(pfx + "a4" + name, shape, dt))
            wo_q = [sb("wo_q%d" % i, [128, 16, 512], BF16) for i in range(2)]
            xr = [sb("xr%d" % i, [128, 512], F32) for i in range(3)]
            psO = [st4.enter_context(nc.psum_tensor(pfx + "a4o%d" % i, [128, 512], F32)) for i in range(2)]
            b_wo = [Buf("wo0"), Buf("wo1")]
            b_xr = [Buf("xr%d" % i) for i in range(3)]
            b_pso = [Buf("pso0"), Buf("pso1")]
            b_out = Buf("out")
            xn = [0]
            for nq in range(4):
                wi = nq % 2
                P.dma([('pool', lambda e, nq=nq, wi=wi: e.dma_start(out=wo_q[wi][:], in_=wo[:, nq * 512:(nq + 1) * 512].rearrange("(k p) c -> p k c", p=128)))],
                      'wo%d' % wi, w=[b_wo[wi]])
                for tt in range(NTQ):
                    pi = tt % 2
                    ri = xn[0] % 3
                    xn[0] += 1
                    for kc in range(16):
                        P.op('pe', lambda e, kc=kc, tt=tt, pi=pi, wi=wi: e.matmul(psO[pi][:], lhsT=oT_all[:, kc, tt * 128:(tt + 1) * 128],
                                                                                   rhs=wo_q[wi][:, kc, :], start=(kc == 0), stop=(kc == 15)),
                             r=[b_hT, b_wo[wi]], w=[b_pso[pi]])
                    P.dma([('sp', lambda e, tt=tt, nq=nq, ri=ri: e.dma_start(out=xr[ri][:], in_=x_in[tt * 128:(tt + 1) * 128, nq * 512:(nq + 1) * 512]))],
                          'xr%d' % ri, w=[b_xr[ri]])
                    P.op('dve', lambda e, pi=pi, ri=ri: e.tensor_tensor(out=xr[ri][:], in0=psO[pi][:], in1=xr[ri][:], op=ALU.add),
                         r=[b_pso[pi], b_xr[ri]], w=[b_xr[ri]])
                    P.dma([('sp', lambda e, tt=tt, nq=nq, ri=ri: e.dma_start(out=x_out[tt * 128:(tt + 1) * 128, nq * 512:(nq + 1) * 512], in_=xr[ri][:]))],
                          'xw%d' % ri, r=[b_xr[ri]], w=[b_out])
            P.flush(semstack)


S_LEN = 4096
BF = ml_dtypes.bfloat16


def own_rows(half):
    own = np.arange(half * 2048, half * 2048 + 2048)
    halo = np.arange(2048, 2176) if half == 0 else np.arange(1920, 2048)
    return np.concatenate([own, halo])


def fourier_tables(rows):
    c = np.arange(256)
    ang = 2 * np.pi * ((np.outer(c, c)) % 256) / 256
    ccsc = np.concatenate([np.cos(ang), np.sin(ang)], axis=1) / 16.0
    s = np.arange(S_LEN)
    ang2 = 2 * np.pi * ((np.outer(s, rows)) % S_LEN) / S_LEN
    cs = np.stack([np.cos(ang2) / 64.0, -np.sin(ang2) / 64.0], axis=1)
    return ccsc.astype(BF), cs.astype(BF)


def attn_cols():
    cols = []
    for p in range(4):
        for c in range(4):
            for hq in (8 * p + c, 8 * p + 4 + c):
                cols.append(np.arange(hq * 64, hq * 64 + 64))
    for p in range(4):
        for kvh in (2 * p, 2 * p + 1):
            cols.append(2048 + np.arange(kvh * 64, kvh * 64 + 64))
    cols = np.concatenate(cols)
    swap = cols.reshape(-1, 2, 32)[:, ::-1, :].reshape(-1)
    return cols, swap


def rope_tables(rows):
    inv_freq = (10000.0 ** (-np.arange(0, 64, 2, dtype=np.float32) / 64)).astype(np.float32)
    ang = rows.astype(np.float32)[:, None] * inv_freq[None, :]
    cos = np.cos(ang).astype(np.float32).T
    sin = np.sin(ang).astype(np.float32).T
    cos_t = np.concatenate([cos, cos, cos, cos], axis=0)
    sin_t = np.concatenate([-sin, sin, -sin, sin], axis=0)
    return np.ascontiguousarray(cos_t), np.ascontiguousarray(sin_t)


def attn_mask(half):
    m = np.full((16, 128, 384), -30000.0, np.float32)
    qi = np.arange(128)[:, None]
    si = np.arange(384)[None, :]
    band = (si >= qi) & (si <= qi + 256)
    for n in range(16):
        valid = np.ones(384, bool)
        if n == 0 and half == 0:
            valid[0:128] = False
        if n == 15 and half == 1:
            valid[256:384] = False
        m[n][band & valid[None, :]] = 0.0
    return m


def chunk_uT(u):
    return np.ascontiguousarray(u.reshape(128, 128, 16, 128).transpose(0, 3, 2, 1)).reshape(128, 128, 2048)


def build_program():
    from contextlib import ExitStack
    nc = bass.Bass("TRN2", target_bir_lowering=False)
    di = lambda n, s, d=F32: nc.dram_tensor(n, s, d, kind="ExternalInput").ap()
    xseq = di("xseq", [S_LEN, D])
    xown = di("xown", [17 * 128, D])
    ccsc = di("ccsc", [256, 512], BF16)
    cs = di("cs", [S_LEN, 2, 17 * 128], BF16)
    g_mix0, g_mix1, g_ffn0, g_ffn1, g_fin = (di(n, [D]) for n in ("g_mix0", "g_mix1", "g_ffn0", "g_ffn1", "g_fin"))
    fwo = di("fwo", [D, D])
    wqk = di("wqk", [D, 2560])
    wqk_sw = di("wqk_sw", [D, 2560])
    wv = di("wv", [D, 512])
    awo = di("awo", [D, D])
    sinks = di("sinks", [32])
    cos_t = di("cos_t", [128, 17 * 128])
    sin_t = di("sin_t", [128, 17 * 128])
    amask = di("amask", [16, 128, 384])
    wq = [di("wq%d" % i, [D, D]) for i in range(2)]
    skT = [di("skT%d" % i, [16, 128, 128]) for i in range(2)]
    uT = [di("uT%d" % i, [128, 128, 2048]) for i in range(2)]
    vv = [di("v%d" % i, [NE, D]) for i in range(2)]
    ident_d = di("ident", [128, 128])
    iota_d = di("iota", [128, 128])
    out = nc.dram_tensor("out", [2048, D], F32, kind="ExternalOutput").ap()
    x1 = nc.dram_tensor("x1", [17 * 128, D], F32, kind="Internal").ap()
    x2 = nc.dram_tensor("x2", [17 * 128, D], F32, kind="Internal").ap()
    x3 = nc.dram_tensor("x3", [2048, D], F32, kind="Internal").ap()
    wg = nc.dram_tensor("wg", [17, 128, 128, 128], BF16, kind="Internal").ap()
    u_bf = [nc.dram_tensor("u_bf%d" % i, [128, 128, 2048], BF16, kind="Internal").ap() for i in range(2)]
    wq_bfd = [nc.dram_tensor("wq_bfd%d" % i, [16, 128, 16, 128], BF16, kind="Internal").ap() for i in range(2)]
    v_bfd = [nc.dram_tensor("v_bfd%d" % i, [NE, D], BF16, kind="Internal").ap() for i in range(2)]
    P = Prog(nc)
    with ExitStack() as semstack:
        with ExitStack() as st:
            fourier_phase(nc, P, st, 17, xseq, xown, x1, g_mix0, ccsc, cs, fwo, ident_d, conv_args=(uT[0], vv[0], u_bf[0], v_bfd[0], wq[0], wq_bfd[0]))
            P.flush(semstack)
        with ExitStack() as st:
            peer_phase(nc, P, st, 17, x1, x2, g_ffn0, wq_bfd[0].rearrange("h p a b -> h p (a b)"), skT[0], u_bf[0], v_bfd[0], wg, ident_d, iota_d,
                       conv_args=(uT[1], vv[1], u_bf[1], v_bfd[1], wq[1], wq_bfd[1]))
            P.flush(semstack)
        attn_phase(nc, P, semstack, x2, x3, g_mix1, wqk, wqk_sw, wv, awo, sinks, cos_t, sin_t, amask, ident_d,
                   conv_args=None)
        with ExitStack() as st:
            peer_phase(nc, P, st, 16, x3, out, g_ffn1, wq_bfd[1].rearrange("h p a b -> h p (a b)"), skT[1], u_bf[1], v_bfd[1], wg, ident_d, iota_d, final_g=g_fin)
            P.flush(semstack)
    return nc


def kernel(x, mix_norm, ffn_norm, fourier_w_o, attn_w_qkv, attn_w_o, attn_sinks,
           peer_w_q, peer_sub_keys, peer_u, peer_v, final_norm):
    f = lambda a: np.ascontiguousarray(np.asarray(a, dtype=np.float32))
    x = f(x)
    mix_norm, ffn_norm, final_norm = f(mix_norm), f(ffn_norm), f(final_norm)
    w_qkv = f(attn_w_qkv)[0]
    cols, swap = attn_cols()
    shared = {
        "g_mix0": f(mix_norm[0]), "g_mix1": f(mix_norm[1]), "g_ffn0": f(ffn_norm[0]), "g_ffn1": f(ffn_norm[1]),
        "g_fin": final_norm, "fwo": f(fourier_w_o)[0],
        "wqk": np.ascontiguousarray(w_qkv[:, cols]), "wqk_sw": np.ascontiguousarray(w_qkv[:, swap]),
        "wv": np.ascontiguousarray(w_qkv[:, 2560:]), "awo": f(attn_w_o)[0], "sinks": f(attn_sinks)[0],
        "ident": np.eye(128, dtype=np.float32), "iota": np.tile(np.arange(128, dtype=np.float32), (128, 1)),
    }
    pu, pv, pwq, psk = f(peer_u), f(peer_v), f(peer_w_q), f(peer_sub_keys)
    for l in range(2):
        shared["wq%d" % l] = pwq[l]
        shared["skT%d" % l] = np.ascontiguousarray(psk[l].reshape(16, 128, 128).transpose(0, 2, 1))
        shared["uT%d" % l] = chunk_uT(pu[l])
        shared["v%d" % l] = pv[l]
    per_half = []
    for half in range(2):
        rows = own_rows(half)
        ccsc, cs = fourier_tables(rows)
        cos_t, sin_t = rope_tables(rows)
        per_half.append({"rows": rows, "ccsc": ccsc, "cs": cs, "cos_t": cos_t, "sin_t": sin_t, "amask": attn_mask(half)})
    in_maps = []
    for core in range(8):
        b, half = core // 2, core % 2
        ph = per_half[half]
        m = dict(shared)
        m["xseq"] = x[b]
        m["xown"] = np.ascontiguousarray(x[b][ph["rows"]])
        for k in ("ccsc", "cs", "cos_t", "sin_t", "amask"):
            m[k] = ph[k]
        in_maps.append(m)
    nc = build_program()
    res = run_bass_kernel_spmd(nc, in_maps, core_ids=list(range(8)))
    outp = np.empty((4, S_LEN, D), np.float32)
    for core in range(8):
        b, half = core // 2, core % 2
        outp[b, half * 2048:(half + 1) * 2048] = res.results[core]["out"]
    return outp
```

```python
import numpy as np
import ml_dtypes
import concourse.bass as bass
import concourse.mybir as mybir
from concourse.bass_utils import run_bass_kernel_spmd

F32 = mybir.dt.float32
BF16 = mybir.dt.bfloat16
U32 = mybir.dt.uint32
ALU = mybir.AluOpType
AF = mybir.ActivationFunctionType
AX = mybir.AxisListType

D = 2048
NE = 16384
EPS = 1e-6
ENGS = ('pe', 'act', 'dve', 'pool', 'sp')


class Buf:
    __slots__ = ('name', 'w', 'r')

    def __init__(self, name):
        self.name = name
        self.w = None
        self.r = {}


class Prog:
    def __init__(self, nc):
        self.nc = nc
        self.q = {e: [] for e in ENGS}
        self.cnt = {e: 0 for e in ENGS}
        self.dcnt = {}
        self.sem = {}
        self.waited = {e: {} for e in ENGS}

    def _deps(self, r, w):
        deps = {}

        def add(tok):
            if tok is None:
                return
            k = tok[:2]
            if deps.get(k, 0) < tok[2]:
                deps[k] = tok[2]
        for b in r:
            add(b.w)
        for b in w:
            add(b.w)
            for k, v in b.r.items():
                add((k[0], k[1], v))
        return deps

    def _commit(self, r, w, tok):
        k = tok[:2]
        for b in r:
            if b.r.get(k, 0) < tok[2]:
                b.r[k] = tok[2]
        for b in w:
            b.w = tok
            b.r = {}

    def op(self, eng, fn, r=(), w=()):
        deps = self._deps(r, w)
        self.cnt[eng] += 1
        tok = ('e', eng, self.cnt[eng])
        self.q[eng].append((fn, deps, tok[:2], 1))
        self._commit(r, w, tok)
        return tok

    def dma(self, parts, key, r=(), w=()):
        deps = self._deps(r, w)
        c = self.dcnt.get(key, 0)
        for eng, fn in parts:
            c += 16
            self.q[eng].append((fn, deps, ('d', key), 16))
        self.dcnt[key] = c
        tok = ('d', key, c)
        self._commit(r, w, tok)
        return tok

    def get_sem(self, k, stack):
        if k not in self.sem:
            self.sem[k] = stack.enter_context(self.nc.semaphore("s_%s_%s" % (k[0], str(k[1]))))
        return self.sem[k]

    def flush(self, stack, barrier=True):
        nc = self.nc
        for e in ENGS:
            self.get_sem(('e', e), stack)
        for e in ENGS:
            for fn, deps, tk, inc in self.q[e]:
                self.get_sem(tk, stack)
        final = {('e', e): self.cnt[e] for e in ENGS}
        for k, v in self.dcnt.items():
            final[('d', k)] = v
        queues = self.q
        self.q = {e: [] for e in ENGS}

        def run(eng_name, eh):
            waited = self.waited[eng_name]
            for fn, deps, tk, inc in queues[eng_name]:
                for k, v in deps.items():
                    if k == ('e', 'pe') and eng_name == 'pe':
                        continue
                    if waited.get(k, 0) >= v:
                        continue
                    eh.wait_ge(self.sem[k], v)
                    waited[k] = v
                ins = fn(eh)
                ins.then_inc(self.sem[tk], inc)
            if barrier:
                for k, v in final.items():
                    if v > 0 and waited.get(k, 0) < v and k != ('e', eng_name):
                        eh.wait_ge(self.sem[k], v)
                        waited[k] = v

        with nc.Block() as block:
            @block.tensor
            def _(eh):
                run('pe', eh)

            @block.scalar
            def _(eh):
                run('act', eh)

            @block.vector
            def _(eh):
                run('dve', eh)

            @block.gpsimd
            def _(eh):
                run('pool', eh)

            @block.sync
            def _(eh):
                run('sp', eh)


def mkap(t, offset, dims):
    base = t[:] if not isinstance(t, bass.AP) else t
    return bass.AP(base.tensor, base.offset + offset, [list(base.ap[0])] + [list(d) for d in dims])


class Converter:
    def __init__(self, nc, P, stack, uT, v, u_bf, v_bf, wq, wq_bf, ncalls, R=4):
        self.P = P
        self.jobs = []
        for dc in range(16):
            self.jobs.append((wq[dc * 128:(dc + 1) * 128, :].rearrange("p (h c) -> p h c", c=128),
                              wq_bf[:, :, dc, :].rearrange("h p c -> p h c")))
        for i in range(128):
            self.jobs.append((uT[i], u_bf[i]))
            self.jobs.append((v[i * 128:(i + 1) * 128, :], v_bf[i * 128:(i + 1) * 128, :]))
        self.i = 0
        self.rate = len(self.jobs) / float(ncalls)
        self.accum = 0.0

    def step(self, n):
        P = self.P
        for _ in range(n):
            if self.i >= len(self.jobs):
                return
            src, dst = self.jobs[self.i]
            k = self.i % 4
            self.i += 1
            P.dma([('pool', lambda e, src=src, dst=dst: e.dma_start(out=dst, in_=src))], 'cv%d' % k)

    def tick(self):
        self.accum += self.rate
        n = int(self.accum)
        self.accum -= n
        self.step(n)

    def finish(self):
        self.step(len(self.jobs))


def peer_phase(nc, P, stack, NT, x_in, x_out, gnorm, wq, skT, uT, v, wg, ident_d, iota_d,
               final_g=None, conv_args=None):
    pfx = "pr%d_" % P.cnt['pe']
    sb = lambda name, shape, dt: stack.enter_context(nc.sbuf_tensor(pfx + name, shape, dt))
    ps = lambda name, shape, dt: stack.enter_context(nc.psum_tensor(pfx + name, shape, dt))
    TB = 4
    SC = 8
    NUB, NVB, NGB = 3, 12, 3
    acc = sb("acc", [128, TB, D], F32)
    g_bc = sb("g_bc", [128, D], F32)
    fg_bc = sb("fg_bc", [128, D], F32) if final_g is not None else None
    ident = sb("ident", [128, 128], F32)
    iota = sb("iota", [128, 128], F32)
    xt = sb("xt", [128, D], F32)
    i16 = sb("i16", [128, 16], F32)
    cif = sb("cif", [128, 8, 16], F32)
    h_f = sb("h_f", [128, D], F32)
    hT = sb("hT", [128, 16, TB * 128], BF16)
    wq_bf = [sb("wq_bf%d" % i, [128, 16, 128], BF16) for i in range(2)]
    sk_bf = sb("sk_bf", [128, 16, 128], BF16)
    s_sb = sb("s_sb", [128, 16, 128], F32)
    s2_sb = sb("s2_sb", [128, 16, 128], F32)
    cand = s_sb[:].rearrange("p (h a) b -> p h (a b)", h=8)
    cand2 = s2_sb[:].rearrange("p (h a) b -> p h (a b)", h=8)
    tv = sb("tv", [128, 16, 16], F32)
    ti = sb("ti", [128, 16, 16], U32)
    tif = sb("tif", [128, 16, 16], F32)
    cv = sb("cv", [128, 8, 16], F32)
    ci = sb("ci", [128, 8, 16], U32)
    jf = sb("jf", [128, 8, 16], F32)
    kf = sb("kf", [128, 8, 16], F32)
    oh = xt[:].rearrange("p (a b) -> p a b", b=16)
    sel = sb("sel", [128, 3, 128], F32)
    selT = sb("selT", [128, 3, 128], F32)
    ssq = sb("ssq", [128, 8], F32)
    zz = sb("zz", [128, 16], F32)
    uT_bf = [sb("uT_bf%d" % i, [128, 16, 128], BF16) for i in range(NUB)]
    v_all = sb("v_all", [128, NVB, D], BF16)
    v_bf = [v_all[:, i, :] for i in range(NVB)]
    A_q = [v_all[:, 2 * i:2 * i + 2, :].rearrange("p a (b c) -> p (a b) c", c=128) for i in range(2)]
    B_q = [v_all[:, 4 + 2 * i:6 + 2 * i, :].rearrange("p a (b c) -> p (a b) c", c=128) for i in range(2)]
    Wst = acc[:].rearrange("p a b -> p (a b)").bitcast(BF16).rearrange("p (i t) -> p i t", i=128)
    gate = [sb("gate%d" % i, [128, TB, 128], BF16) for i in range(NGB)]
    ga = [sb("ga%d" % i, [128, TB * 128], BF16) for i in range(2)]
    PTall = sb("PTall", [128, 2 * SC, TB * 128], BF16)
    PT = [PTall[:, i * SC:(i + 1) * SC, :] for i in range(2)]
    qT_sb = PTall
    psA = ps("psA", [128, 2048], F32)
    psB = ps("psB", [128, 1024], F32)
    psC = ps("psC", [128, 1024], F32)

    b_acc = [Buf("acc%d" % j) for j in range(TB)]
    b_c = Buf("consts")
    b_xt, b_hf, b_hT, b_sk = Buf("xt"), Buf("hf"), Buf("hT"), Buf("sk")
    b_wq = [Buf("wq%d" % i) for i in range(2)]
    b_s, b_s2, b_tk, b_sel, b_selT = Buf("s"), Buf("s2"), Buf("tk"), Buf("sel"), Buf("selT")
    TK = [b_tk, b_s, b_s2, b_xt]
    b_ssq = Buf("ssq")
    b_uT = [Buf("uT%d" % i) for i in range(NUB)]
    b_v = [Buf("v%d" % i) for i in range(NVB)]
    b_gate = [Buf("gate%d" % i) for i in range(NGB)]
    b_ga = [Buf("ga%d" % i) for i in range(2)]
    b_PT = [Buf("PT%d" % i) for i in range(2)]
    b_psA = [Buf("psA%d" % i) for i in range(4)]
    b_psB = [Buf("psB%d" % i) for i in range(2)]
    b_psC = [Buf("psC%d" % i) for i in range(2)]
    b_wg = [Buf("wg%d" % i) for i in range(NT)]
    b_xo = [Buf("xo%d" % i) for i in range(NT)]

    P.dma([('sp', lambda e: e.dma_start(out=g_bc[:], in_=gnorm.partition_broadcast(128)))], 'c0', w=[b_c])
    if final_g is not None:
        P.dma([('sp', lambda e: e.dma_start(out=fg_bc[:], in_=final_g.partition_broadcast(128)))], 'c0', w=[b_c])
    P.dma([('sp', lambda e: e.dma_start(out=ident[:], in_=ident_d))], 'c0', w=[b_c])
    P.dma([('sp', lambda e: e.dma_start(out=iota[:], in_=iota_d))], 'c0', w=[b_c])
    P.op('dve', lambda e: e.tensor_scalar(out=i16[:], in0=iota[:, 0:16], scalar1=16.0, scalar2=None, op0=ALU.mult), r=[b_c], w=[b_c])
    P.dma([('pool', lambda e: e.dma_start(out=sk_bf[:], in_=skT.rearrange("g d k -> d g k")))], 'c1', w=[b_sk])

    wq_n = [0]
    u_n, v_n, g_n, ga_n = [0], [0], [0], [0]
    psb_n, pair_n = [0], [0]

    nblk = (NT + TB - 1) // TB
    conv = None
    if conv_args is not None:
        conv = Converter(nc, P, stack, *conv_args, ncalls=(nblk - 1) * (NE // 128))
    for blk in range(nblk):
        t0 = blk * TB
        nb = min(TB, NT - t0)
        N = nb * 128
        for j in range(nb):
            tt = t0 + j
            P.dma([('sp', lambda e, tt=tt: e.dma_start(out=xt[:], in_=x_in[tt * 128:(tt + 1) * 128, :]))], 'xt', w=[b_xt])
            P.op('act', lambda e: e.activation(out=h_f[:], in_=xt[:], func=AF.Square), r=[b_xt], w=[b_hf])
            P.op('dve', lambda e: e.tensor_reduce(out=ssq[:, 0:1], in_=h_f[:], axis=AX.X, op=ALU.add), r=[b_hf], w=[b_ssq])
            P.op('dve', lambda e: e.tensor_scalar(out=ssq[:, 1:2], in0=ssq[:, 0:1], scalar1=1.0 / D, scalar2=EPS,
                                                  op0=ALU.mult, op1=ALU.add), r=[b_ssq], w=[b_ssq])
            P.op('act', lambda e: e.activation(out=ssq[:, 2:3], in_=ssq[:, 1:2], func=AF.Sqrt), r=[b_ssq], w=[b_ssq])
            P.op('dve', lambda e: e.reciprocal(out=ssq[:, 3:4], in_=ssq[:, 2:3]), r=[b_ssq], w=[b_ssq])
            P.op('dve', lambda e: e.scalar_tensor_tensor(out=h_f[:], in0=xt[:], scalar=ssq[:, 3:4],
                                                         in1=g_bc[:], op0=ALU.mult, op1=ALU.mult),
                 r=[b_xt, b_ssq, b_c], w=[b_hf])
            for dc in range(16):
                P.op('pe', lambda e, dc=dc: e.transpose(psA[:, dc * 128:(dc + 1) * 128], h_f[:, dc * 128:(dc + 1) * 128], ident[:]),
                     r=[b_hf, b_c], w=[b_psA[dc // 4]])
            P.op('act', lambda e, j=j: e.activation(out=hT[:, :, j * 128:(j + 1) * 128],
                                                     in_=psA[:].rearrange("p (a b) -> p a b", a=16), func=AF.Copy),
                 r=b_psA, w=[b_hT])
        for hp in range(16):
            wi = wq_n[0] % 2
            wq_n[0] += 1
            P.dma([('sp', lambda e, hp=hp, wi=wi: e.dma_start(out=wq_bf[wi][:], in_=wq[hp].rearrange("p (a b) -> p a b", b=128)))],
                  'wq%d' % wi, w=[b_wq[wi]])
            bi = psb_n[0] % 2
            psb_n[0] += 1
            for dc in range(16):
                P.op('pe', lambda e, dc=dc, wi=wi, bi=bi, N=N: e.matmul(psB[:, bi * 512:bi * 512 + N], lhsT=wq_bf[wi][:, dc, :],
                                                                         rhs=hT[:, dc, 0:N], start=(dc == 0), stop=(dc == 15)),
                     r=[b_wq[wi], b_hT], w=[b_psB[bi]])
            P.op('act', lambda e, hp=hp, bi=bi, N=N: e.activation(out=qT_sb[:, hp, 0:N], in_=psB[:, bi * 512:bi * 512 + N], func=AF.Copy),
                 r=[b_psB[bi]], w=b_PT)
        def stageA(j):
            tt = t0 + j
            for hp in range(16):
                P.op('pe', lambda e, hp=hp, j=j: e.matmul(psA[:, hp * 128:(hp + 1) * 128], lhsT=qT_sb[:, hp, j * 128:(j + 1) * 128],
                                                          rhs=sk_bf[:, hp, :], start=True, stop=True),
                     r=b_PT + [b_sk], w=[b_psA[hp // 4]])
            P.op('act', lambda e: e.activation(out=s_sb[:].rearrange("p a b -> p (a b)"), in_=psA[:], func=AF.Copy),
                 r=b_psA, w=[b_s])
        def stageB(j):
            tt = t0 + j
            for g in range(16):
                P.op('dve', lambda e, g=g: e.max(out=tv[:, g, 0:8], in_=s_sb[:, g, :]), r=[b_s], w=[b_tk])
                P.op('dve', lambda e, g=g: e.max_index(out=ti[:, g, 0:8], in_max=tv[:, g, 0:8], in_values=s_sb[:, g, :]),
                     r=[b_s, b_tk], w=[b_tk])
                P.op('dve', lambda e, g=g: e.match_replace(out=s2_sb[:, g, :], in_to_replace=tv[:, g, 0:8],
                                                           in_values=s_sb[:, g, :], imm_value=-1e30), r=[b_s, b_tk], w=[b_s2])
                P.op('dve', lambda e, g=g: e.max(out=tv[:, g, 8:16], in_=s2_sb[:, g, :]), r=[b_s2], w=[b_tk])
                P.op('dve', lambda e, g=g: e.max_index(out=ti[:, g, 8:16], in_max=tv[:, g, 8:16], in_values=s2_sb[:, g, :]),
                     r=[b_s2, b_tk], w=[b_tk])
            P.op('dve', lambda e: e.tensor_copy(out=tif[:], in_=ti[:]), r=[b_tk], w=[b_tk])
            P.op('dve', lambda e: e.tensor_tensor(out=cand[:].rearrange("p h (j k) -> p h j k", j=16),
                                                  in0=mkap(tv, 0, [[32, 8], [1, 16], [0, 16]]),
                                                  in1=mkap(tv, 16, [[32, 8], [0, 16], [1, 16]]), op=ALU.add),
                 r=TK, w=TK)
            for h in range(8):
                P.op('dve', lambda e, h=h: e.max(out=cv[:, h, 0:8], in_=cand[:, h, :]), r=TK, w=TK)
                P.op('dve', lambda e, h=h: e.max_index(out=ci[:, h, 0:8], in_max=cv[:, h, 0:8], in_values=cand[:, h, :]),
                     r=TK, w=TK)
                P.op('dve', lambda e, h=h: e.match_replace(out=cand2[:, h, :], in_to_replace=cv[:, h, 0:8],
                                                           in_values=cand[:, h, :], imm_value=-1e30), r=TK, w=TK)
                P.op('dve', lambda e, h=h: e.max(out=cv[:, h, 8:16], in_=cand2[:, h, :]), r=TK, w=TK)
                P.op('dve', lambda e, h=h: e.max_index(out=ci[:, h, 8:16], in_max=cv[:, h, 8:16], in_values=cand2[:, h, :]),
                     r=TK, w=TK)
            P.op('dve', lambda e: e.tensor_copy(out=cif[:], in_=ci[:]), r=TK, w=TK)
            P.op('dve', lambda e: e.tensor_tensor(out=oh[:], in0=mkap(cif, 0, [[1, 128], [0, 16]]),
                                                  in1=mkap(i16, 0, [[0, 128], [1, 16]]), op=ALU.is_ge), r=TK + [b_c], w=TK)
            P.op('dve', lambda e: e.tensor_reduce(out=jf[:].rearrange("p h c -> p (h c)"), in_=oh[:], axis=AX.X, op=ALU.add),
                 r=TK, w=TK)
            P.op('dve', lambda e: e.tensor_scalar(out=jf[:], in0=jf[:], scalar1=-1.0, scalar2=None, op0=ALU.add), r=TK, w=TK)
            P.op('dve', lambda e: e.scalar_tensor_tensor(out=kf[:].rearrange("p h c -> p (h c)"), in0=jf[:].rearrange("p h c -> p (h c)"),
                                                         scalar=-16.0, in1=cif[:].rearrange("p h c -> p (h c)"), op0=ALU.mult, op1=ALU.add),
                 r=TK, w=TK)
            for which, src, off in ((0, jf, 0), (1, kf, 16)):
                P.op('dve', lambda e, src=src: e.tensor_tensor(out=oh[:], in0=mkap(src, 0, [[1, 128], [0, 16]]),
                                                               in1=mkap(iota, 0, [[0, 128], [1, 16]]), op=ALU.is_equal),
                     r=TK + [b_c], w=TK)
                P.op('dve', lambda e, off=off: e.tensor_tensor(out=oh[:].rearrange("p (h c) j -> p h c j", h=8),
                                                               in0=oh[:].rearrange("p (h c) j -> p h c j", h=8),
                                                               in1=mkap(tif, off, [[32, 8], [0, 16], [1, 16]]), op=ALU.mult),
                     r=TK, w=TK)
                P.op('dve', lambda e, which=which: e.tensor_reduce(out=sel[:, which, :], in_=oh[:], axis=AX.X, op=ALU.add),
                     r=TK + [b_selT], w=TK + [b_sel])
            P.op('dve', lambda e: e.tensor_tensor(out=jf[:], in0=cv[:], in1=mkap(cv, 0, [[16, 8], [0, 16]]), op=ALU.subtract),
                 r=TK, w=TK)
            P.op('act', lambda e: e.activation(out=kf[:], in_=jf[:], func=AF.Exp), r=TK, w=TK)
            P.op('dve', lambda e: e.tensor_reduce(out=zz[:, 0:8], in_=kf[:], axis=AX.X, op=ALU.add), r=TK, w=TK)
            P.op('dve', lambda e: e.reciprocal(out=zz[:, 8:16], in_=zz[:, 0:8]), r=TK, w=TK)
            P.op('dve', lambda e: e.tensor_tensor(out=sel[:, 2, :].rearrange("p (h c) -> p h c", h=8), in0=kf[:],
                                                  in1=mkap(zz, 8, [[1, 8], [0, 16]]), op=ALU.mult),
                 r=TK, w=TK + [b_sel])
            for k3 in range(3):
                P.op('pe', lambda e, k3=k3: e.transpose(psC[:, k3 * 128:(k3 + 1) * 128], sel[:, k3, :], ident[:]),
                     r=[b_sel, b_c], w=[b_psC[0]])
            P.op('act', lambda e: e.activation(out=selT[:].rearrange("p a b -> p (a b)"), in_=psC[:, 0:384], func=AF.Copy),
                 r=[b_psC[0]], w=[b_selT])
        def stageC(j):
            tt = t0 + j
            for qd in range(4):
                ab = qd % 2
                P.op('dve', lambda e, ab=ab, qd=qd: e.tensor_tensor(out=A_q[ab][:], in0=mkap(iota, 0, [[0, 32], [1, 128]]),
                                                                      in1=mkap(selT, qd * 32, [[1, 32], [0, 128]]), op=ALU.is_equal),
                     r=[b_selT, b_c], w=[b_v[2 * ab], b_v[2 * ab + 1]])
                P.op('dve', lambda e, ab=ab, qd=qd: e.tensor_tensor(out=B_q[ab][:], in0=mkap(iota, 0, [[0, 32], [1, 128]]),
                                                                     in1=mkap(selT, 128 + qd * 32, [[1, 32], [0, 128]]), op=ALU.is_equal),
                     r=[b_selT, b_c], w=[b_v[4 + 2 * ab], b_v[5 + 2 * ab]])
                P.op('dve', lambda e, ab=ab, qd=qd: e.tensor_tensor(out=B_q[ab][:], in0=B_q[ab][:],
                                                                     in1=mkap(selT, 256 + qd * 32, [[1, 32], [0, 128]]), op=ALU.mult),
                     r=[b_selT], w=[b_v[4 + 2 * ab], b_v[5 + 2 * ab]])
                for t8 in range(4):
                    for tq in range(8):
                        tl = t8 * 8 + tq
                        P.op('pe', lambda e, ab=ab, tl=tl, tq=tq: e.matmul(psB[:, tq * 128:(tq + 1) * 128], lhsT=B_q[ab][:, tl, :],
                                                                            rhs=A_q[ab][:, tl, :], start=True, stop=True),
                             r=[b_v[2 * ab], b_v[2 * ab + 1], b_v[4 + 2 * ab], b_v[5 + 2 * ab]], w=[b_psB[tq // 4]])
                    tb = qd * 32 + t8 * 8
                    P.op('act', lambda e, tb=tb: e.activation(out=Wst[:, :, tb:tb + 8],
                                                               in_=psB[:].rearrange("p (t i) -> p i t", t=8), func=AF.Copy),
                         r=b_psB, w=b_acc)
            P.dma([('sp', lambda e, tt=tt: e.dma_start(out=wg[tt].rearrange("p i t -> p (i t)"),
                                                       in_=Wst.rearrange("p i t -> p (i t)")))],
                  'wst', r=b_acc, w=[b_wg[tt]])
        stageA(0)
        stageB(0)
        for j in range(nb):
            if j + 1 < nb:
                stageA(j + 1)
            stageC(j)
            if j + 1 < nb:
                stageB(j + 1)
        for j in range(nb):
            P.dma([('sp', lambda e, j=j, tt=t0 + j: e.dma_start(out=acc[:, j, :], in_=x_in[tt * 128:(tt + 1) * 128, :]))],
                  'x%d' % j, w=[b_acc[j]])
        for sc in range(NE // 128 // SC):
            pt = sc % 2
            vslots = []
            for cix in range(SC):
                i1 = sc * SC + cix
                ui = u_n[0] % NUB
                u_n[0] += 1
                vi = v_n[0] % NVB
                v_n[0] += 1
                gi = g_n[0] % NGB
                g_n[0] += 1
                gai = ga_n[0] % 2
                ga_n[0] += 1
                bi = psb_n[0] % 2
                psb_n[0] += 1
                vslots.append(vi)
                if conv is not None:
                    conv.tick()
                P.dma([('sp', lambda e, ui=ui, i1=i1: e.dma_start(out=uT_bf[ui][:].rearrange("p a b -> p (a b)"), in_=uT[i1]))],
                      'u%d' % ui, w=[b_uT[ui]])
                P.dma([('sp', lambda e, vi=vi, i1=i1: e.dma_start(out=v_bf[vi][:], in_=v[i1 * 128:(i1 + 1) * 128, :]))],
                      'v%d' % vi, w=[b_v[vi]])
                P.dma([('sp', lambda e, gi=gi, i1=i1, t0=t0, nb=nb: e.dma_start(
                    out=gate[gi][:, 0:nb, :], in_=wg[t0:t0 + nb, :, i1, :].rearrange("n p t -> p n t")))],
                    'g%d' % gi, r=[b_wg[t0 + jj] for jj in range(nb)], w=[b_gate[gi]])
                for dc in range(16):
                    P.op('pe', lambda e, dc=dc, ui=ui, bi=bi, N=N: e.matmul(psB[:, bi * 512:bi * 512 + N], lhsT=uT_bf[ui][:, dc, :],
                                                                             rhs=hT[:, dc, 0:N], start=(dc == 0), stop=(dc == 15)),
                         r=[b_uT[ui], b_hT], w=[b_psB[bi]])
                P.op('act', lambda e, gai=gai, bi=bi, N=N: e.activation(out=ga[gai][:, 0:N], in_=psB[:, bi * 512:bi * 512 + N],
                                                                         func=AF.Gelu_apprx_tanh),
                     r=[b_psB[bi]], w=[b_ga[gai]])
                P.op('dve', lambda e, gai=gai, gi=gi, pt=pt, cix=cix, N=N: e.tensor_tensor(
                    out=PT[pt][:, cix, 0:N], in0=ga[gai][:, 0:N], in1=gate[gi][:].rearrange("p a b -> p (a b)")[:, 0:N], op=ALU.mult),
                    r=[b_ga[gai], b_gate[gi]], w=[b_PT[pt]])
            for j in range(nb):
                for half in range(2):
                    pr = pair_n[0] % 3
                    pair_n[0] += 1
                    if pr < 2:
                        pst, pbufs, pbase = psA, [b_psA[2 * pr], b_psA[2 * pr + 1]], pr * 1024
                    else:
                        pst, pbufs, pbase = psC, b_psC, 0
                    for cix in range(SC):
                        for nq in range(2):
                            col = half * 1024 + nq * 512
                            P.op('pe', lambda e, pst=pst, pbase=pbase, nq=nq, pt=pt, cix=cix, j=j, col=col, vi=vslots[cix]: e.matmul(
                                pst[:, pbase + nq * 512:pbase + (nq + 1) * 512], lhsT=PT[pt][:, cix, j * 128:(j + 1) * 128],
                                rhs=v_bf[vi][:, col:col + 512], start=(cix == 0), stop=(cix == SC - 1)),
                                r=[b_PT[pt], b_v[vslots[cix]]], w=[pbufs[nq]])
                    P.op('dve', lambda e, pst=pst, pbase=pbase, j=j, half=half: e.tensor_tensor(
                        out=acc[:, j, half * 1024:(half + 1) * 1024], in0=pst[:, pbase:pbase + 1024],
                        in1=acc[:, j, half * 1024:(half + 1) * 1024], op=ALU.add),
                        r=pbufs + [b_acc[j]], w=[b_acc[j]])
        for j in range(nb):
            tt = t0 + j
            if final_g is not None:
                P.op('act', lambda e, j=j: e.activation(out=h_f[:], in_=acc[:, j, :], func=AF.Square), r=[b_acc[j]], w=[b_hf])
                P.op('dve', lambda e: e.tensor_reduce(out=ssq[:, 4:5], in_=h_f[:], axis=AX.X, op=ALU.add), r=[b_hf], w=[b_ssq])
                P.op('dve', lambda e: e.tensor_scalar(out=ssq[:, 5:6], in0=ssq[:, 4:5], scalar1=1.0 / D, scalar2=EPS,
                                                      op0=ALU.mult, op1=ALU.add), r=[b_ssq], w=[b_ssq])
                P.op('act', lambda e: e.activation(out=ssq[:, 6:7], in_=ssq[:, 5:6], func=AF.Sqrt), r=[b_ssq], w=[b_ssq])
                P.op('dve', lambda e: e.reciprocal(out=ssq[:, 7:8], in_=ssq[:, 6:7]), r=[b_ssq], w=[b_ssq])
                P.op('dve', lambda e, j=j: e.scalar_tensor_tensor(out=acc[:, j, :], in0=acc[:, j, :], scalar=ssq[:, 7:8],
                                                                  in1=fg_bc[:], op0=ALU.mult, op1=ALU.mult),
                     r=[b_acc[j], b_ssq, b_c], w=[b_acc[j]])
            P.dma([('sp', lambda e, j=j, tt=tt: e.dma_start(out=x_out[tt * 128:(tt + 1) * 128, :], in_=acc[:, j, :]))],
                  'xo%d' % j, r=[b_acc[j]], w=[b_xo[tt]])
    if conv is not None:
        conv.finish()


def fourier_phase(nc, P, stack, NT, xseq, xown, x_out, gnorm, ccsc, cs, wo, ident_d, conv_args=None):
    pfx = "fr%d_" % P.cnt['pe']
    sb = lambda name, shape, dt: stack.enter_context(nc.sbuf_tensor(pfx + name, shape, dt))
    ps = lambda name, shape, dt: stack.enter_context(nc.psum_tensor(pfx + name, shape, dt))
    NS = 32
    NSL = 6
    g_bc = sb("g_bc", [128, D], F32)
    ident = sb("ident", [128, 128], F32)
    xfull = sb("xfull", [128, D], F32)
    junk = sb("junk", [128, D], F32)
    ssq = sb("ssq", [128, NS], F32)
    rstd = sb("rstd", [128, NS], F32)
    cc_bf = sb("cc_bf", [128, 2, 512], BF16)
    xs = [sb("xs%d" % i, [128, 512], F32) for i in range(2)]
    h_f = sb("h_f", [128, 512], F32)
    hTt = sb("hTt", [128, 4, 128], BF16)
    G_sb = sb("G_sb", [128, 2, NS, 512], BF16)
    slab = [sb("slab%d" % i, [128, 2, 512], BF16) for i in range(NSL)]
    yT = sb("yT", [128, 4, NT * 128], BF16)
    wo_bf = sb("wo_bf", [128, 4, D], BF16)
    xr = [sb("xr%d" % i, [128, D], F32) for i in range(2)]
    psA = ps("psA", [128, 2048], F32)
    psB = ps("psB", [128, 1024], F32)
    psC = ps("psC", [128, 512], F32)
    b_c, b_xfull, b_junk, b_ssq, b_cc = Buf("c"), Buf("xfull"), Buf("junk"), Buf("ssq"), Buf("cc")
    b_xs = [Buf("xs%d" % i) for i in range(2)]
    b_hf, b_hTt, b_G, b_yT, b_wo = Buf("hf"), Buf("hTt"), Buf("G"), Buf("yT"), Buf("wo")
    b_slab = [Buf("slab%d" % i) for i in range(NSL)]
    b_xr = [Buf("xr%d" % i) for i in range(2)]
    b_psA = [Buf("psA%d" % i) for i in range(4)]
    b_psB, b_psC = Buf("psB"), Buf("psC")
    b_x1 = [Buf("x1_%d" % i) for i in range(NT)]

    P.dma([('sp', lambda e: e.dma_start(out=g_bc[:], in_=gnorm.partition_broadcast(128)))], 'c0', w=[b_c])
    P.dma([('sp', lambda e: e.dma_start(out=ident[:], in_=ident_d))], 'c0', w=[b_c])
    P.dma([('sp', lambda e: e.dma_start(out=cc_bf[:], in_=ccsc.rearrange("(k p) c -> p k c", p=128)))], 'c1', w=[b_cc])
    for st in range(NS):
        P.dma([('sp', lambda e, st=st: e.dma_start(out=xfull[:], in_=xseq[st * 128:(st + 1) * 128, :]))], 'xf', w=[b_xfull])
        P.op('act', lambda e: e.activation(out=junk[:], in_=xfull[:], func=AF.Square), r=[b_xfull], w=[b_junk])
        P.op('dve', lambda e, st=st: e.tensor_reduce(out=ssq[:, st:st + 1], in_=junk[:], axis=AX.X, op=ALU.add), r=[b_junk], w=[b_ssq])
    P.op('dve', lambda e: e.tensor_scalar(out=ssq[:], in0=ssq[:], scalar1=1.0 / D, scalar2=EPS, op0=ALU.mult, op1=ALU.add),
         r=[b_ssq], w=[b_ssq])
    P.op('act', lambda e: e.activation(out=ssq[:], in_=ssq[:], func=AF.Sqrt), r=[b_ssq], w=[b_ssq])
    P.op('dve', lambda e: e.reciprocal(out=rstd[:], in_=ssq[:]), r=[b_ssq], w=[b_ssq])
    conv = None
    if conv_args is not None:
        conv = Converter(nc, P, stack, *conv_args, ncalls=4 * NS + 4 * NS * 5 - 40)
    sblocks = []
    s0 = 0
    while s0 < NT * 128:
        n = min(512, NT * 128 - s0)
        sblocks.append((s0, n))
        s0 += n
    xs_n, sl_n, xr_n = [0], [0], [0]
    for cp in range(4):
        c0 = cp * 512
        P.dma([('pool', lambda e, c0=c0: e.dma_start(out=wo_bf[:], in_=wo[c0:c0 + 512, :].rearrange("(k p) c -> p k c", p=128)))],
              'wo', w=[b_wo])
        for st in range(NS):
            xi = xs_n[0] % 2
            xs_n[0] += 1
            P.dma([('sp', lambda e, st=st, xi=xi, c0=c0: e.dma_start(out=xs[xi][:], in_=xseq[st * 128:(st + 1) * 128, c0:c0 + 512]))],
                  'xs%d' % xi, w=[b_xs[xi]])
            P.op('dve', lambda e, st=st, xi=xi, c0=c0: e.scalar_tensor_tensor(out=h_f[:], in0=xs[xi][:], scalar=rstd[:, st:st + 1],
                                                                             in1=g_bc[:, c0:c0 + 512], op0=ALU.mult, op1=ALU.mult),
                 r=[b_xs[xi], b_ssq, b_c], w=[b_hf])
            for k in range(4):
                P.op('pe', lambda e, k=k: e.transpose(psC[:, k * 128:(k + 1) * 128], h_f[:, k * 128:(k + 1) * 128], ident[:]),
                     r=[b_hf, b_c], w=[b_psC])
            P.op('act', lambda e: e.activation(out=hTt[:].rearrange("p a b -> p (a b)"), in_=psC[:], func=AF.Copy), r=[b_psC], w=[b_hTt])
            for gg in range(2):
                for which in range(2):
                    for kc in range(2):
                        P.op('pe', lambda e, gg=gg, which=which, kc=kc: e.matmul(
                            psB[:, (gg * 2 + which) * 256:(gg * 2 + which + 1) * 256], lhsT=hTt[:, gg * 2 + kc, :],
                            rhs=cc_bf[:, kc, which * 256:(which + 1) * 256], start=(kc == 0), stop=(kc == 1)),
                            r=[b_hTt, b_cc], w=[b_psB])
            if conv is not None:
                conv.tick()
            for which in range(2):
                P.op('act', lambda e, st=st, which=which: e.activation(
                    out=G_sb[:, which, st, :].rearrange("p (g c) -> p g c", g=2),
                    in_=psB[:].rearrange("p (g w c) -> p g w c", g=2, w=2)[:, :, which, :], func=AF.Copy),
                    r=[b_psB], w=[b_G])
        for (s0, n) in sblocks:
            for st in range(NS):
                si = sl_n[0] % NSL
                sl_n[0] += 1
                P.dma([('sp', lambda e, st=st, si=si, s0=s0, n=n: e.dma_start(out=slab[si][:, :, 0:n],
                                                                              in_=cs[st * 128:(st + 1) * 128, :, s0:s0 + n]))],
                      'sl%d' % si, w=[b_slab[si]])
                if conv is not None:
                    conv.tick()
                for dq in range(4):
                    for which in range(2):
                        P.op('pe', lambda e, st=st, si=si, dq=dq, which=which, n=n: e.matmul(
                            psA[:, dq * 512:dq * 512 + n], lhsT=G_sb[:, which, st, dq * 128:(dq + 1) * 128],
                            rhs=slab[si][:, which, 0:n], start=(st == 0 and which == 0), stop=(st == NS - 1 and which == 1)),
                            r=[b_G, b_slab[si]], w=[b_psA[dq]])
            for dq in range(4):
                P.op('act', lambda e, dq=dq, s0=s0, n=n: e.activation(out=yT[:, dq, s0:s0 + n], in_=psA[:, dq * 512:dq * 512 + n], func=AF.Copy),
                     r=[b_psA[dq]], w=[b_yT])
        for tt in range(NT):
            for nq in range(4):
                for kc in range(4):
                    P.op('pe', lambda e, tt=tt, nq=nq, kc=kc: e.matmul(psA[:, nq * 512:(nq + 1) * 512], lhsT=yT[:, kc, tt * 128:(tt + 1) * 128],
                                                                        rhs=wo_bf[:, kc, nq * 512:(nq + 1) * 512], start=(kc == 0), stop=(kc == 3)),
                         r=[b_yT, b_wo], w=[b_psA[nq]])
            ri = xr_n[0] % 2
            xr_n[0] += 1
            src = xown if cp == 0 else x_out
            P.dma([('sp', lambda e, tt=tt, ri=ri, src=src: e.dma_start(out=xr[ri][:], in_=src[tt * 128:(tt + 1) * 128, :]))],
                  'xr%d' % ri, r=[b_x1[tt]], w=[b_xr[ri]])
            P.op('dve', lambda e, ri=ri: e.tensor_tensor(out=xr[ri][:], in0=psA[:], in1=xr[ri][:], op=ALU.add),
                 r=b_psA + [b_xr[ri]], w=[b_xr[ri]])
            P.dma([('sp', lambda e, tt=tt, ri=ri: e.dma_start(out=x_out[tt * 128:(tt + 1) * 128, :], in_=xr[ri][:]))],
                  'xw%d' % ri, r=[b_xr[ri]], w=[b_x1[tt]])
    if conv is not None:
        conv.finish()


def attn_phase(nc, P, semstack, x_in, x_out, gnorm, wqk, wqk_sw, wv, wo, sinks, cos_t, sin_t, amask, ident_d, conv_args=None):
    from contextlib import ExitStack
    NTK, NTQ = 17, 16
    pfx = "at%d_" % P.cnt['pe']
    with ExitStack() as outer:
        sbo = lambda name, shape, dt: outer.enter_context(nc.sbuf_tensor(pfx + name, shape, dt))
        hT_all = sbo("hT_all", [128, 16, NTK * 128], BF16)
        oT_all = hT_all
        qT_all = sbo("qT_all", [128, 16, NTQ * 128], BF16)
        kT_all = sbo("kT_all", [128, 4, NTK * 128], BF16)
        v_all = sbo("v_all", [128, NTK, 512], BF16)
        ident = sbo("ident", [128, 128], F32)
        ident_bf = sbo("ident_bf", [128, 128], BF16)
        b_hT, b_q, b_k, b_v, b_c = Buf("hT"), Buf("q"), Buf("k"), Buf("v"), Buf("c")
        with ExitStack() as st1:
            sb = lambda name, shape, dt: st1.enter_context(nc.sbuf_tensor(pfx + "a1" + name, shape, dt))
            g_bc = sb("g_bc", [128, D], F32)
            xt = [sb("xt%d" % i, [128, D], F32) for i in range(2)]
            h_f = sb("h_f", [128, D], F32)
            ssq = sb("ssq", [128, 4], F32)
            psA = st1.enter_context(nc.psum_tensor(pfx + "a1psA", [128, 2048], F32))
            b_xt = [Buf("xt0"), Buf("xt1")]
            b_hf, b_ssq = Buf("hf"), Buf("ssq")
            b_psA = [Buf("psA%d" % i) for i in range(4)]
            P.dma([('sp', lambda e: e.dma_start(out=g_bc[:], in_=gnorm.partition_broadcast(128)))], 'c0', w=[b_c])
            P.dma([('sp', lambda e: e.dma_start(out=ident[:], in_=ident_d))], 'c0', w=[b_c])
            P.op('dve', lambda e: e.tensor_copy(out=ident_bf[:], in_=ident[:]), r=[b_c], w=[b_c])
            for tt in range(NTK):
                xi = tt % 2
                P.dma([('sp', lambda e, tt=tt, xi=xi: e.dma_start(out=xt[xi][:], in_=x_in[tt * 128:(tt + 1) * 128, :]))],
                      'xt%d' % xi, w=[b_xt[xi]])
                P.op('act', lambda e, xi=xi: e.activation(out=h_f[:], in_=xt[xi][:], func=AF.Square), r=[b_xt[xi]], w=[b_hf])
                P.op('dve', lambda e: e.tensor_reduce(out=ssq[:, 0:1], in_=h_f[:], axis=AX.X, op=ALU.add), r=[b_hf], w=[b_ssq])
                P.op('dve', lambda e: e.tensor_scalar(out=ssq[:, 1:2], in0=ssq[:, 0:1], scalar1=1.0 / D, scalar2=EPS,
                                                      op0=ALU.mult, op1=ALU.add), r=[b_ssq], w=[b_ssq])
                P.op('act', lambda e: e.activation(out=ssq[:, 2:3], in_=ssq[:, 1:2], func=AF.Sqrt), r=[b_ssq], w=[b_ssq])
                P.op('dve', lambda e: e.reciprocal(out=ssq[:, 3:4], in_=ssq[:, 2:3]), r=[b_ssq], w=[b_ssq])
                P.op('dve', lambda e, xi=xi: e.scalar_tensor_tensor(out=h_f[:], in0=xt[xi][:], scalar=ssq[:, 3:4], in1=g_bc[:],
                                                                    op0=ALU.mult, op1=ALU.mult), r=[b_xt[xi], b_ssq, b_c], w=[b_hf])
                for dc in range(16):
                    P.op('pe', lambda e, dc=dc: e.transpose(psA[:, dc * 128:(dc + 1) * 128], h_f[:, dc * 128:(dc + 1) * 128], ident[:]),
                         r=[b_hf, b_c], w=[b_psA[dc // 4]])
                P.op('act', lambda e, tt=tt: e.activation(out=hT_all[:, :, tt * 128:(tt + 1) * 128],
                                                           in_=psA[:].rearrange("p (a b) -> p a b", a=16), func=AF.Copy),
                     r=b_psA, w=[b_hT])
            P.flush(semstack)
        with ExitStack() as st2:
            sb = lambda name, shape, dt: st2.enter_context(nc.sbuf_tensor(pfx + "a2" + name, shape, dt))
            cosb = sb("cosb", [128, NTK * 128], F32)
            sinb = sb("sinb", [128, NTK * 128], F32)
            wch = [sb("wch%d" % i, [128, 16, 128], BF16) for i in range(2)]
            wsw = [sb("wsw%d" % i, [128, 16, 128], BF16) for i in range(2)]
            tmp1 = sb("tmp1", [128, 512], F32)
            tmp2 = sb("tmp2", [128, 512], F32)
            psX = st2.enter_context(nc.psum_tensor(pfx + "a2psX", [128, 1024], F32))
            psY = st2.enter_context(nc.psum_tensor(pfx + "a2psY", [128, 1024], F32))
            psV = st2.enter_context(nc.psum_tensor(pfx + "a2psV", [128, 1024], F32))
            b_tab, b_wv, b_t1, b_t2 = Buf("tab"), Buf("wv"), Buf("t1"), Buf("t2")
            b_w = [Buf("w0"), Buf("w1")]
            b_psX, b_psY, b_psV = [Buf("x0"), Buf("x1")], [Buf("y0"), Buf("y1")], [Buf("v0"), Buf("v1")]
            P.dma([('sp', lambda e: e.dma_start(out=cosb[:], in_=cos_t))], 'c0', w=[b_tab])
            P.dma([('sp', lambda e: e.dma_start(out=sinb[:], in_=sin_t))], 'c0', w=[b_tab])
            wr = wqk.rearrange("(k p) c -> p k c", p=128)
            wsr = wqk_sw.rearrange("(k p) c -> p k c", p=128)
            pn = [0]
            for ch in list(range(16, 20)) + list(range(16)):
                wi = ch % 2
                P.dma([('pool', lambda e, ch=ch, wi=wi: e.dma_start(out=wch[wi][:], in_=wr[:, :, ch * 128:(ch + 1) * 128]))],
                      'wa%d' % wi, w=[b_w[wi]])
                P.dma([('pool', lambda e, ch=ch, wi=wi: e.dma_start(out=wsw[wi][:], in_=wsr[:, :, ch * 128:(ch + 1) * 128]))],
                      'wb%d' % wi, w=[b_w[wi]])
                ntok = (NTK if ch >= 16 else NTQ) * 128
                g0 = 0
                while g0 < ntok:
                    n = min(512, ntok - g0)
                    pi = pn[0] % 2
                    pn[0] += 1
                    for dc in range(16):
                        P.op('pe', lambda e, dc=dc, wi=wi, pi=pi, g0=g0, n=n: e.matmul(psX[:, pi * 512:pi * 512 + n], lhsT=wch[wi][:, dc, :],
                                                                                      rhs=hT_all[:, dc, g0:g0 + n], start=(dc == 0), stop=(dc == 15)),
                             r=[b_w[wi], b_hT], w=[b_psX[pi]])
                    for dc in range(16):
                        P.op('pe', lambda e, dc=dc, wi=wi, pi=pi, g0=g0, n=n: e.matmul(psY[:, pi * 512:pi * 512 + n], lhsT=wsw[wi][:, dc, :],
                                                                                      rhs=hT_all[:, dc, g0:g0 + n], start=(dc == 0), stop=(dc == 15)),
                             r=[b_w[wi], b_hT], w=[b_psY[pi]])
                    P.op('dve', lambda e, pi=pi, g0=g0, n=n: e.tensor_tensor(out=tmp1[:, 0:n], in0=psX[:, pi * 512:pi * 512 + n],
                                                                             in1=cosb[:, g0:g0 + n], op=ALU.mult),
                         r=[b_psX[pi], b_tab], w=[b_t1])
                    P.op('dve', lambda e, pi=pi, g0=g0, n=n: e.tensor_tensor(out=tmp2[:, 0:n], in0=psY[:, pi * 512:pi * 512 + n],
                                                                             in1=sinb[:, g0:g0 + n], op=ALU.mult),
                         r=[b_psY[pi], b_tab], w=[b_t2])
                    if ch >= 16:
                        P.op('dve', lambda e, ch=ch, g0=g0, n=n: e.tensor_tensor(out=kT_all[:, ch - 16, g0:g0 + n], in0=tmp1[:, 0:n],
                                                                                 in1=tmp2[:, 0:n], op=ALU.add), r=[b_t1, b_t2], w=[b_k])
                    else:
                        P.op('dve', lambda e, ch=ch, g0=g0, n=n: e.tensor_tensor(out=qT_all[:, ch, g0:g0 + n], in0=tmp1[:, 0:n],
                                                                                 in1=tmp2[:, 0:n], op=ALU.add), r=[b_t1, b_t2], w=[b_q])
                    g0 += n
            wvr = wv.rearrange("(k p) c -> p k c", p=128)
            for vc in range(4):
                wi = vc % 2
                P.dma([('pool', lambda e, vc=vc, wi=wi: e.dma_start(out=wch[wi][:], in_=wvr[:, :, vc * 128:(vc + 1) * 128]))],
                      'wa%d' % wi, w=[b_w[wi]])
                for tt in range(NTK):
                    pi = tt % 2
                    for dc in range(16):
                        P.op('pe', lambda e, dc=dc, tt=tt, pi=pi, wi=wi: e.matmul(psV[:, pi * 512:pi * 512 + 128], lhsT=hT_all[:, dc, tt * 128:(tt + 1) * 128],
                                                                                   rhs=wch[wi][:, dc, :], start=(dc == 0), stop=(dc == 15)),
                             r=[b_hT, b_w[wi]], w=[b_psV[pi]])
                    P.op('act', lambda e, tt=tt, pi=pi, vc=vc: e.activation(out=v_all[:, tt, vc * 128:(vc + 1) * 128], in_=psV[:, pi * 512:pi * 512 + 128], func=AF.Copy),
                         r=[b_psV[pi]], w=[b_v])
            P.flush(semstack)
        with ExitStack() as st3:
            sb = lambda name, shape, dt: st3.enter_context(nc.sbuf_tensor(pfx + "a3" + name, shape, dt))
            sink_bc = sb("sink_bc", [128, 32], F32)
            m_sb = [sb("m_sb%d" % i, [128, 384], F32) for i in range(2)]
            sm = [sb("sm%d" % i, [128, 4, 384], F32) for i in range(2)]
            p_bf = [sb("p_bf%d" % i, [128, 4, 384], BF16) for i in range(2)]
            pT_sb = [sb("pT_sb%d" % i, [128, 4, 384], BF16) for i in range(2)]
            stt = [sb("stt%d" % i, [128, 8, 4], F32) for i in range(2)]
            o_blk = [sb("o_blk%d" % i, [128, 512], F32) for i in range(2)]
            s_ps = st3.enter_context(nc.psum_tensor(pfx + "a3s", [128, 4, 512], F32))
            pT_ps = st3.enter_context(nc.psum_tensor(pfx + "a3pT", [128, 4, 512], BF16))
            o_ps = st3.enter_context(nc.psum_tensor(pfx + "a3o", [128, 512], F32))
            psT = st3.enter_context(nc.psum_tensor(pfx + "a3psT", [128, 512], F32))
            b_sink = Buf("sink")
            b_m = [Buf("m0"), Buf("m1")]
            b_sm = [Buf("sm%d" % i) for i in range(2)]
            b_p = [Buf("p%d" % i) for i in range(2)]
            b_pT = [Buf("pT%d" % i) for i in range(2)]
            b_st = [Buf("st%d" % i) for i in range(2)]
            b_ob = [Buf("ob0"), Buf("ob1")]
            b_sps = [Buf("sps%d" % i) for i in range(4)]
            b_pTps = [Buf("pTps0"), Buf("pTps1")]
            b_ops, b_psT = Buf("ops"), Buf("psT")
            P.dma([('sp', lambda e: e.dma_start(out=sink_bc[:], in_=sinks.partition_broadcast(128)))], 'c0', w=[b_sink])
            conv = None
            if conv_args is not None:
                conv = Converter(nc, P, st3, *conv_args, ncalls=60, R=4)
            groups = []
            for blk in range(NTQ):
                for p in range(4):
                    for par in range(2):
                        groups.append((blk, p, par))

            def ktiles(blk):
                return [blk - 1 if blk > 0 else 16, blk, blk + 1 if blk < 15 else 16]

            def s1(g):
                blk, p, par = groups[g]
                r2, mi = g % 2, blk % 2
                kt = ktiles(blk)
                if p == 0 and par == 0:
                    P.dma([('sp', lambda e: e.dma_start(out=m_sb[mi][:], in_=amask[blk]))], 'mk%d' % mi, w=[b_m[mi]])
                lo, hi = par * 64, par * 64 + 64
                h0 = 8 * p + 4 * par
                for c in range(4):
                    for j in range(3):
                        P.op('pe', lambda e, j=j, c=c, ktj=kt[j]: e.matmul(
                            s_ps[:, c, j * 128:(j + 1) * 128], lhsT=qT_all[lo:hi, p * 4 + c, blk * 128:(blk + 1) * 128],
                            rhs=kT_all[lo:hi, p, ktj * 128:(ktj + 1) * 128], start=True, stop=True),
                            r=[b_q, b_k], w=[b_sps[c]])
                P.op('dve', lambda e: e.tensor_tensor(out=sm[r2][:], in0=s_ps[:, :, 0:384],
                                                      in1=mkap(m_sb[mi], 0, [[0, 4], [1, 384]]), op=ALU.add),
                     r=b_sps + [b_m[mi]], w=[b_sm[r2]])
                P.op('dve', lambda e: e.tensor_reduce(out=stt[r2][:, 0, :], in_=sm[r2][:], axis=AX.X, op=ALU.max),
                     r=[b_sm[r2]], w=[b_st[r2]])
                P.op('dve', lambda e: e.scalar_tensor_tensor(out=stt[r2][:, 1, :], in0=stt[r2][:, 0, :], scalar=0.125,
                                                             in1=sink_bc[:, h0:h0 + 4], op0=ALU.mult, op1=ALU.max),
                     r=[b_st[r2], b_sink], w=[b_st[r2]])
                P.op('dve', lambda e: e.tensor_scalar(out=stt[r2][:, 2, :], in0=stt[r2][:, 1, :], scalar1=-1.0, scalar2=None,
                                                      op0=ALU.mult), r=[b_st[r2]], w=[b_st[r2]])
                P.op('dve', lambda e: e.tensor_tensor(out=stt[r2][:, 3, :], in0=sink_bc[:, h0:h0 + 4], in1=stt[r2][:, 2, :],
                                                      op=ALU.add), r=[b_st[r2], b_sink], w=[b_st[r2]])

            def s2(g):
                blk, p, par = groups[g]
                r2 = g % 2
                for c in range(4):
                    P.op('act', lambda e, c=c: e.activation(out=p_bf[r2][:, c, :], in_=sm[r2][:, c, :], func=AF.Exp,
                                                            bias=stt[r2][:, 2, c:c + 1], scale=0.125),
                         r=[b_sm[r2], b_st[r2]], w=[b_p[r2]])
                P.op('act', lambda e: e.activation(out=stt[r2][:, 4, :], in_=stt[r2][:, 3, :], func=AF.Exp),
                     r=[b_st[r2]], w=[b_st[r2]])
                P.op('dve', lambda e: e.tensor_reduce(out=stt[r2][:, 5, :], in_=p_bf[r2][:], axis=AX.X, op=ALU.add),
                     r=[b_p[r2]], w=[b_st[r2]])
                P.op('dve', lambda e: e.tensor_tensor(out=stt[r2][:, 6, :], in0=stt[r2][:, 5, :], in1=stt[r2][:, 4, :], op=ALU.add),
                     r=[b_st[r2]], w=[b_st[r2]])
                P.op('dve', lambda e: e.reciprocal(out=stt[r2][:, 7, :], in_=stt[r2][:, 6, :]), r=[b_st[r2]], w=[b_st[r2]])
                for c in range(4):
                    for j in range(3):
                        P.op('pe', lambda e, j=j, c=c: e.transpose(pT_ps[:, c, j * 128:(j + 1) * 128],
                                                                   p_bf[r2][:, c, j * 128:(j + 1) * 128], ident_bf[:]),
                             r=[b_p[r2], b_c], w=[b_pTps[c // 2]])
                P.op('act', lambda e: e.activation(out=pT_sb[r2][:], in_=pT_ps[:, :, 0:384], func=AF.Copy),
                     r=b_pTps, w=[b_pT[r2]])

            def s3(g):
                blk, p, par = groups[g]
                r2 = g % 2
                kt = ktiles(blk)
                ob = (blk * 4 + p) % 2
                vc = (2 * p + par) * 64
                for c in range(4):
                    for j in range(3):
                        P.op('pe', lambda e, j=j, c=c, ktj=kt[j]: e.matmul(
                            o_ps[:, c * 64:(c + 1) * 64], lhsT=pT_sb[r2][:, c, j * 128:(j + 1) * 128], rhs=v_all[:, ktj, vc:vc + 64],
                            start=(j == 0), stop=(j == 2)), r=[b_pT[r2], b_v], w=[b_ops])
                P.op('dve', lambda e: e.tensor_tensor(
                    out=o_blk[ob][:, par * 256:(par + 1) * 256].rearrange("p (c d) -> p c d", c=4),
                    in0=o_ps[:, 0:256].rearrange("p (c d) -> p c d", c=4),
                    in1=mkap(stt[r2], 28, [[1, 4], [0, 64]]), op=ALU.mult),
                    r=[b_ops, b_st[r2]], w=[b_ob[ob]])
                if par == 1:
                    for k in range(4):
                        P.op('pe', lambda e, k=k: e.transpose(psT[:, k * 128:(k + 1) * 128], o_blk[ob][:, k * 128:(k + 1) * 128], ident[:]),
                             r=[b_ob[ob], b_c], w=[b_psT])
                    P.op('act', lambda e: e.activation(out=oT_all[:, p * 4:(p + 1) * 4, blk * 128:(blk + 1) * 128],
                                                       in_=psT[:].rearrange("p (a b) -> p a b", a=4), func=AF.Copy),
                         r=[b_psT], w=[b_hT])

            s1(0)
            for g in range(len(groups)):
                if g + 1 < len(groups):
                    s1(g + 1)
                s2(g)
                s3(g)
            if conv is not None:
                conv.finish()
            P.flush(semstack)
        with ExitStack() as st4:
            sb = lambda name, shape, dt: st4.enter_context(nc.sbuf_tensor(pfx + "a4" + name, shape, dt))
            wo_q = [sb("wo_q%d" % i, [128, 16, 512], BF16) for i in range(2)]
            xr = [sb("xr%d" % i, [128, 512], F32) for i in range(3)]
            psO = [st4.enter_context(nc.psum_tensor(pfx + "a4o%d" % i, [128, 512], F32)) for i in range(2)]
            b_wo = [Buf("wo0"), Buf("wo1")]
            b_xr = [Buf("xr%d" % i) for i in range(3)]
            b_pso = [Buf("pso0"), Buf("pso1")]
            b_out = Buf("out")
            xn = [0]
            for nq in range(4):
                wi = nq % 2
                P.dma([('pool', lambda e, nq=nq, wi=wi: e.dma_start(out=wo_q[wi][:], in_=wo[:, nq * 512:(nq + 1) * 512].rearrange("(k p) c -> p k c", p=128)))],
                      'wo%d' % wi, w=[b_wo[wi]])
                for tt in range(NTQ):
                    pi = tt % 2
                    ri = xn[0] % 3
                    xn[0] += 1
                    for kc in range(16):
                        P.op('pe', lambda e, kc=kc, tt=tt, pi=pi, wi=wi: e.matmul(psO[pi][:], lhsT=oT_all[:, kc, tt * 128:(tt + 1) * 128],
                                                                                   rhs=wo_q[wi][:, kc, :], start=(kc == 0), stop=(kc == 15)),
                             r=[b_hT, b_wo[wi]], w=[b_pso[pi]])
                    P.dma([('sp', lambda e, tt=tt, nq=nq, ri=ri: e.dma_start(out=xr[ri][:], in_=x_in[tt * 128:(tt + 1) * 128, nq * 512:(nq + 1) * 512]))],
                          'xr%d' % ri, w=[b_xr[ri]])
                    P.op('dve', lambda e, pi=pi, ri=ri: e.tensor_tensor(out=xr[ri][:], in0=psO[pi][:], in1=xr[ri][:], op=ALU.add),
                         r=[b_pso[pi], b_xr[ri]], w=[b_xr[ri]])
                    P.dma([('sp', lambda e, tt=tt, nq=nq, ri=ri: e.dma_start(out=x_out[tt * 128:(tt + 1) * 128, nq * 512:(nq + 1) * 512], in_=xr[ri][:]))],
                          'xw%d' % ri, r=[b_xr[ri]], w=[b_out])
            P.flush(semstack)


S_LEN = 4096
BF = ml_dtypes.bfloat16


def own_rows(half):
    own = np.arange(half * 2048, half * 2048 + 2048)
    halo = np.arange(2048, 2176) if half == 0 else np.arange(1920, 2048)
    return np.concatenate([own, halo])


def fourier_tables(rows):
    c = np.arange(256)
    ang = 2 * np.pi * ((np.outer(c, c)) % 256) / 256
    ccsc = np.concatenate([np.cos(ang), np.sin(ang)], axis=1) / 16.0
    s = np.arange(S_LEN)
    ang2 = 2 * np.pi * ((np.outer(s, rows)) % S_LEN) / S_LEN
    cs = np.stack([np.cos(ang2) / 64.0, -np.sin(ang2) / 64.0], axis=1)
    return ccsc.astype(BF), cs.astype(BF)


def attn_cols():
    cols = []
    for p in range(4):
        for c in range(4):
            for hq in (8 * p + c, 8 * p + 4 + c):
                cols.append(np.arange(hq * 64, hq * 64 + 64))
    for p in range(4):
        for kvh in (2 * p, 2 * p + 1):
            cols.append(2048 + np.arange(kvh * 64, kvh * 64 + 64))
    cols = np.concatenate(cols)
    swap = cols.reshape(-1, 2, 32)[:, ::-1, :].reshape(-1)
    return cols, swap


def rope_tables(rows):
    inv_freq = (10000.0 ** (-np.arange(0, 64, 2, dtype=np.float32) / 64)).astype(np.float32)
    ang = rows.astype(np.float32)[:, None] * inv_freq[None, :]
    cos = np.cos(ang).astype(np.float32).T
    sin = np.sin(ang).astype(np.float32).T
    cos_t = np.concatenate([cos, cos, cos, cos], axis=0)
    sin_t = np.concatenate([-sin, sin, -sin, sin], axis=0)
    return np.ascontiguousarray(cos_t), np.ascontiguousarray(sin_t)


def attn_mask(half):
    m = np.full((16, 128, 384), -30000.0, np.float32)
    qi = np.arange(128)[:, None]
    si = np.arange(384)[None, :]
    band = (si >= qi) & (si <= qi + 256)
    for n in range(16):
        valid = np.ones(384, bool)
        if n == 0 and half == 0:
            valid[0:128] = False
        if n == 15 and half == 1:
            valid[256:384] = False
        m[n][band & valid[None, :]] = 0.0
    return m


def chunk_uT(u):
    return np.ascontiguousarray(u.reshape(128, 128, 16, 128).transpose(0, 3, 2, 1)).reshape(128, 128, 2048)


def build_program():
    from contextlib import ExitStack
    nc = bass.Bass("TRN2", target_bir_lowering=False)
    di = lambda n, s, d=F32: nc.dram_tensor(n, s, d, kind="ExternalInput").ap()
    xseq = di("xseq", [S_LEN, D])
    xown = di("xown", [17 * 128, D])
    ccsc = di("ccsc", [256, 512], BF16)
    cs = di("cs", [S_LEN, 2, 17 * 128], BF16)
    g_mix0, g_mix1, g_ffn0, g_ffn1, g_fin = (di(n, [D]) for n in ("g_mix0", "g_mix1", "g_ffn0", "g_ffn1", "g_fin"))
    fwo = di("fwo", [D, D])
    wqk = di("wqk", [D, 2560])
    wqk_sw = di("wqk_sw", [D, 2560])
    wv = di("wv", [D, 512])
    awo = di("awo", [D, D])
    sinks = di("sinks", [32])
    cos_t = di("cos_t", [128, 17 * 128])
    sin_t = di("sin_t", [128, 17 * 128])
    amask = di("amask", [16, 128, 384])
    wq = [di("wq%d" % i, [D, D]) for i in range(2)]
    skT = [di("skT%d" % i, [16, 128, 128]) for i in range(2)]
    uT = [di("uT%d" % i, [128, 128, 2048]) for i in range(2)]
    vv = [di("v%d" % i, [NE, D]) for i in range(2)]
    ident_d = di("ident", [128, 128])
    iota_d = di("iota", [128, 128])
    out = nc.dram_tensor("out", [2048, D], F32, kind="ExternalOutput").ap()
    x1 = nc.dram_tensor("x1", [17 * 128, D], F32, kind="Internal").ap()
    x2 = nc.dram_tensor("x2", [17 * 128, D], F32, kind="Internal").ap()
    x3 = nc.dram_tensor("x3", [2048, D], F32, kind="Internal").ap()
    wg = nc.dram_tensor("wg", [17, 128, 128, 128], BF16, kind="Internal").ap()
    u_bf = [nc.dram_tensor("u_bf%d" % i, [128, 128, 2048], BF16, kind="Internal").ap() for i in range(2)]
    wq_bfd = [nc.dram_tensor("wq_bfd%d" % i, [16, 128, 16, 128], BF16, kind="Internal").ap() for i in range(2)]
    v_bfd = [nc.dram_tensor("v_bfd%d" % i, [NE, D], BF16, kind="Internal").ap() for i in range(2)]
    P = Prog(nc)
    with ExitStack() as semstack:
        with ExitStack() as st:
            fourier_phase(nc, P, st, 17, xseq, xown, x1, g_mix0, ccsc, cs, fwo, ident_d, conv_args=(uT[0], vv[0], u_bf[0], v_bfd[0], wq[0], wq_bfd[0]))
            P.flush(semstack)
        with ExitStack() as st:
            peer_phase(nc, P, st, 17, x1, x2, g_ffn0, wq_bfd[0].rearrange("h p a b -> h p (a b)"), skT[0], u_bf[0], v_bfd[0], wg, ident_d, iota_d,
                       conv_args=(uT[1], vv[1], u_bf[1], v_bfd[1], wq[1], wq_bfd[1]))
            P.flush(semstack)
        attn_phase(nc, P, semstack, x2, x3, g_mix1, wqk, wqk_sw, wv, awo, sinks, cos_t, sin_t, amask, ident_d,
                   conv_args=None)
        with ExitStack() as st:
            peer_phase(nc, P, st, 16, x3, out, g_ffn1, wq_bfd[1].rearrange("h p a b -> h p (a b)"), skT[1], u_bf[1], v_bfd[1], wg, ident_d, iota_d, final_g=g_fin)
            P.flush(semstack)
    return nc


def kernel(x, mix_norm, ffn_norm, fourier_w_o, attn_w_qkv, attn_w_o, attn_sinks,
           peer_w_q, peer_sub_keys, peer_u, peer_v, final_norm):
    f = lambda a: np.ascontiguousarray(np.asarray(a, dtype=np.float32))
    x = f(x)
    mix_norm, ffn_norm, final_norm = f(mix_norm), f(ffn_norm), f(final_norm)
    w_qkv = f(attn_w_qkv)[0]
    cols, swap = attn_cols()
    shared = {
        "g_mix0": f(mix_norm[0]), "g_mix1": f(mix_norm[1]), "g_ffn0": f(ffn_norm[0]), "g_ffn1": f(ffn_norm[1]),
        "g_fin": final_norm, "fwo": f(fourier_w_o)[0],
        "wqk": np.ascontiguousarray(w_qkv[:, cols]), "wqk_sw": np.ascontiguousarray(w_qkv[:, swap]),
        "wv": np.ascontiguousarray(w_qkv[:, 2560:]), "awo": f(attn_w_o)[0], "sinks": f(attn_sinks)[0],
        "ident": np.eye(128, dtype=np.float32), "iota": np.tile(np.arange(128, dtype=np.float32), (128, 1)),
    }
    pu, pv, pwq, psk = f(peer_u), f(peer_v), f(peer_w_q), f(peer_sub_keys)
    for l in range(2):
        shared["wq%d" % l] = pwq[l]
        shared["skT%d" % l] = np.ascontiguousarray(psk[l].reshape(16, 128, 128).transpose(0, 2, 1))
        shared["uT%d" % l] = chunk_uT(pu[l])
        shared["v%d" % l] = pv[l]
    per_half = []
    for half in range(2):
        rows = own_rows(half)
        ccsc, cs = fourier_tables(rows)
        cos_t, sin_t = rope_tables(rows)
        per_half.append({"rows": rows, "ccsc": ccsc, "cs": cs, "cos_t": cos_t, "sin_t": sin_t, "amask": attn_mask(half)})
    in_maps = []
    for core in range(8):
        b, half = core // 2, core % 2
        ph = per_half[half]
        m = dict(shared)
        m["xseq"] = x[b]
        m["xown"] = np.ascontiguousarray(x[b][ph["rows"]])
        for k in ("ccsc", "cs", "cos_t", "sin_t", "amask"):
            m[k] = ph[k]
        in_maps.append(m)
    nc = build_program()
    res = run_bass_kernel_spmd(nc, in_maps, core_ids=list(range(8)))
    outp = np.empty((4, S_LEN, D), np.float32)
    for core in range(8):
        b, half = core // 2, core % 2
        outp[b, half * 2048:(half + 1) * 2048] = res.results[core]["out"]
    return outp
```
